# Optimizing a Trainium2 kernel written in Bass

```python
import jax, jax.numpy as jnp
from jax import lax
import numpy as np

D_MODEL = 1024
BATCH = 8
SEQ = 4096
DEPTH = 4

N_META = 16
N_A_LAYERS = DEPTH // 2
N_B_LAYERS = DEPTH - N_A_LAYERS
RET_HEADS = 4
RET_QK_DIM = D_MODEL // RET_HEADS
RET_V_DIM = 2 * D_MODEL // RET_HEADS
RET_CHUNK = 128
RET_PROJ = 2 * RET_HEADS * RET_QK_DIM + 2 * RET_HEADS * RET_V_DIM
SWA_HEADS = 16
SWA_KV_HEADS = 4
SWA_GROUP = SWA_HEADS // SWA_KV_HEADS
SWA_HEAD_DIM = 64
SWA_WINDOW = 128
FFN_HIDDEN = -(-8 * D_MODEL // (3 * 256)) * 256
RMS_EPS = 1e-6
GN_EPS = 1e-6

kernel_name = "retnet_yoco_swa_sink_alibi_meta_trunk"


def rms_norm(x, g):
    xf = x.astype(jnp.float32)
    y = xf * lax.rsqrt(jnp.mean(xf * xf, axis=-1, keepdims=True) + RMS_EPS)
    return (y * g.astype(jnp.float32)).astype(x.dtype)


def swiglu(x, w_in, w_out):
    gate, up = jnp.split(x @ w_in, 2, axis=-1)
    return (jax.nn.silu(gate) * up) @ w_out


def retention_log_decay():
    return jnp.log1p(-jnp.exp2(-5.0 - jnp.arange(RET_HEADS, dtype=jnp.float32)))


def alibi_slopes():
    return jnp.exp2(-8.0 * (jnp.arange(SWA_HEADS, dtype=jnp.float32) + 1.0) / SWA_HEADS)


def retention(x, w_in, w_out):
    B, L, _ = x.shape
    H, dk, dv, C = RET_HEADS, RET_QK_DIM, RET_V_DIM, RET_CHUNK
    pad = C - N_META
    nc = (L + pad) // C
    q, k, v, g = jnp.split(x @ w_in, [H * dk, 2 * H * dk, 2 * H * dk + H * dv], axis=-1)

    def chunks(t, d):
        return jnp.pad(t, ((0, 0), (pad, 0), (0, 0))).reshape(B, nc, C, H, d)

    q = chunks(q, dk) * (dk ** -0.5)
    k = chunks(k, dk)
    v = chunks(v, dv)
    log_gamma = retention_log_decay()
    i = jnp.arange(C, dtype=jnp.float32)
    diff = i[:, None] - i[None, :]
    decay_intra = jnp.where(diff >= 0, jnp.exp(log_gamma[:, None, None] * jnp.maximum(diff, 0.0)), 0.0)
    scores = jnp.einsum('bnchd,bnshd->bnhcs', q, k) * decay_intra
    intra = jnp.einsum('bnhcs,bnshe->bnche', scores, v)
    zeta = jnp.exp(log_gamma[:, None] * (C - 1.0 - i)[None, :])
    xi = jnp.exp(log_gamma[:, None] * (i + 1.0)[None, :])
    chunk_decay = jnp.exp(log_gamma * C)[None, :, None, None]

    def step(state, xs):
        qc, kc, vc = xs
        inter = jnp.einsum('bchd,hc,bhde->bche', qc, xi, state)
        state = state * chunk_decay + jnp.einsum('bshd,hs,bshe->bhde', kc, zeta, vc)
        return state, inter

    state0 = jnp.zeros((B, H, dk, dv), jnp.float32)
    _, inter = lax.scan(step, state0, (jnp.moveaxis(q, 1, 0), jnp.moveaxis(k, 1, 0), jnp.moveaxis(v, 1, 0)))
    o = intra + jnp.moveaxis(inter, 0, 1)
    o = o.reshape(B, nc * C, H, dv)[:, pad:].astype(jnp.float32)
    mu = jnp.mean(o, axis=-1, keepdims=True)
    var = jnp.mean(jnp.square(o - mu), axis=-1, keepdims=True)
    o = ((o - mu) * lax.rsqrt(var + GN_EPS)).astype(x.dtype).reshape(B, L, H * dv)
    return (jax.nn.silu(g) * o) @ w_out


def shared_kv(h, g, w_kv):
    B, L, _ = h.shape
    k, v = jnp.split(rms_norm(h, g) @ w_kv, 2, axis=-1)
    k = k.reshape(B, L, SWA_KV_HEADS, SWA_HEAD_DIM)
    v = v.reshape(B, L, SWA_KV_HEADS, SWA_HEAD_DIM)
    return k[:, :N_META], v[:, :N_META], k[:, N_META:], v[:, N_META:]


def sliding_window_attention(x, w_q, w_o, sinks, k_meta, v_meta, k_real, v_real):
    B, S, _ = x.shape
    W = SWA_WINDOW
    nb = S // W
    q = (x @ w_q).reshape(B, nb, W, SWA_KV_HEADS, SWA_GROUP, SWA_HEAD_DIM) * (SWA_HEAD_DIM ** -0.5)

    def band(t):
        tp = jnp.pad(t, ((0, 0), (W, 0), (0, 0), (0, 0))).reshape(B, nb + 1, W, SWA_KV_HEADS, SWA_HEAD_DIM)
        return jnp.concatenate([tp[:, :-1], tp[:, 1:]], axis=2)

    k_band, v_band = band(k_real), band(v_real)
    slopes = alibi_slopes().reshape(SWA_KV_HEADS, SWA_GROUP)[:, :, None, None, None]
    r = jnp.arange(W)
    c = jnp.arange(2 * W)
    blk = jnp.arange(nb)
    dist = W + r[:, None] - c[None, :]
    mask = (dist >= 0) & (dist < W) & ((blk[:, None, None] * W + c[None, None, :]) >= W)
    s_band = jnp.einsum('bnrkgd,bnckd->bkgnrc', q, k_band).astype(jnp.float32) - slopes * dist.astype(jnp.float32)
    s_band = jnp.where(mask, s_band, -jnp.inf)
    q_pos = N_META + blk[:, None] * W + r[None, :]
    meta_dist = (q_pos[:, :, None] - jnp.arange(N_META)[None, None, :]).astype(jnp.float32)
    s_meta = jnp.einsum('bnrkgd,bmkd->bkgnrm', q, k_meta).astype(jnp.float32) - slopes * meta_dist
    s_sink = sinks.astype(jnp.float32).reshape(SWA_KV_HEADS, SWA_GROUP)[None, :, :, None, None]
    m = jnp.maximum(jnp.maximum(s_band.max(-1), s_meta.max(-1)), s_sink)
    e_band = jnp.exp(s_band - m[..., None])
    e_meta = jnp.exp(s_meta - m[..., None])
    inv = 1.0 / (e_band.sum(-1) + e_meta.sum(-1) + jnp.exp(s_sink - m))
    p_band = (e_band * inv[..., None]).astype(v_band.dtype)
    p_meta = (e_meta * inv[..., None]).astype(v_meta.dtype)
    o = (jnp.einsum('bkgnrc,bnckd->bnrkgd', p_band, v_band)
         + jnp.einsum('bkgnrm,bmkd->bnrkgd', p_meta, v_meta))
    return o.reshape(B, S, SWA_HEADS * SWA_HEAD_DIM).astype(x.dtype) @ w_o


def setup_inputs(seed: int = 0) -> dict:
    key = jax.random.key(seed)
    ks = jax.random.split(key, 14)
    resid_scale = (2.0 * DEPTH) ** -0.5

    def nrm(k, shape, fan_in, scale=1.0):
        return jax.random.normal(k, shape, jnp.float32) * (scale * fan_in ** -0.5)

    def gain(k, shape):
        return 1.0 + 0.02 * jax.random.normal(k, shape, jnp.float32)

    return {
        "x": jax.random.normal(ks[0], (BATCH, SEQ, D_MODEL), jnp.float32),
        "meta_tokens": jax.random.normal(ks[1], (N_META, D_MODEL), jnp.float32),
        "mix_norm": gain(ks[2], (DEPTH, D_MODEL)),
        "ffn_norm": gain(ks[3], (DEPTH, D_MODEL)),
        "ret_w_in": nrm(ks[4], (N_A_LAYERS, D_MODEL, RET_PROJ), D_MODEL),
        "ret_w_out": nrm(ks[5], (N_A_LAYERS, RET_HEADS * RET_V_DIM, D_MODEL), RET_HEADS * RET_V_DIM, resid_scale),
        "kv_norm": gain(ks[6], (D_MODEL,)),
        "kv_w": nrm(ks[7], (D_MODEL, 2 * SWA_KV_HEADS * SWA_HEAD_DIM), D_MODEL),
        "swa_w_q": nrm(ks[8], (N_B_LAYERS, D_MODEL, SWA_HEADS * SWA_HEAD_DIM), D_MODEL),
        "swa_w_o": nrm(ks[9], (N_B_LAYERS, SWA_HEADS * SWA_HEAD_DIM, D_MODEL), SWA_HEADS * SWA_HEAD_DIM, resid_scale),
        "swa_sinks": 0.5 * jax.random.normal(ks[10], (N_B_LAYERS, SWA_HEADS), jnp.float32),
        "ffn_w_in": nrm(ks[11], (DEPTH, D_MODEL, 2 * FFN_HIDDEN), D_MODEL),
        "ffn_w_out": nrm(ks[12], (DEPTH, FFN_HIDDEN, D_MODEL), FFN_HIDDEN, resid_scale),
        "final_norm": gain(ks[13], (D_MODEL,)),
    }


def reference(x, meta_tokens, mix_norm, ffn_norm, ret_w_in, ret_w_out, kv_norm, kv_w,
              swa_w_q, swa_w_o, swa_sinks, ffn_w_in, ffn_w_out, final_norm):
    B = x.shape[0]
    meta = jnp.broadcast_to(meta_tokens.astype(x.dtype)[None], (B, N_META, D_MODEL))
    h = jnp.concatenate([meta, x], axis=1)
    k_meta = v_meta = k_real = v_real = None
    for layer in range(DEPTH):
        if layer < N_A_LAYERS:
            h = h + retention(rms_norm(h, mix_norm[layer]), ret_w_in[layer], ret_w_out[layer])
        else:
            if layer == N_A_LAYERS:
                k_meta, v_meta, k_real, v_real = shared_kv(h, kv_norm, kv_w)
                h = h[:, N_META:]
            b = layer - N_A_LAYERS
            h = h + sliding_window_attention(rms_norm(h, mix_norm[layer]), swa_w_q[b], swa_w_o[b], swa_sinks[b],
                                             k_meta, v_meta, k_real, v_real)
        h = h + swiglu(rms_norm(h, ffn_norm[layer]), ffn_w_in[layer], ffn_w_out[layer])
    return rms_norm(h, final_norm)
```

```python
import numpy as np
import concourse.bass as bass
import concourse.mybir as mybir
from concourse.bass_utils import run_bass_kernel_spmd

F32 = mybir.dt.float32
BF16 = mybir.dt.bfloat16
AF = mybir.ActivationFunctionType
ALU = mybir.AluOpType
AX = mybir.AxisListType

D = 1024
SEQ = 4096
NMETA = 16
RH, DK, DV = 4, 256, 512
FH = 2816
NFC = FH // 128
RMS_EPS = 1e-6
GN_EPS = 1e-6
BIG = 1.0e30
SLAB = 4096

def slab_plan():
    plan = []
    for l in range(2):
        for s in range(2):
            plan.append(("ret", l, "q", s))
        for s in range(2):
            plan.append(("ret", l, "k", s))
        for h in range(4):
            plan.append(("ret", l, "v", h))
        for h in range(4):
            plan.append(("ret", l, "g", h))
        for s in range(4):
            plan.append(("ret", l, "o", s))
        for s in range(11):
            plan.append(("ffn", l, "in", s))
        for s in range(8):
            plan.append(("ffn", l, "out", s))
    plan.append(("kv", 0, "k", 0))
    plan.append(("kv", 0, "v", 0))
    for l in range(2, 4):
        for s in range(2):
            plan.append(("swa", l, "q", s))
        for s in range(2):
            plan.append(("swa", l, "o", s))
        for s in range(11):
            plan.append(("ffn", l, "in", s))
        for s in range(8):
            plan.append(("ffn", l, "out", s))
    return plan


PLAN = slab_plan()
SID = {k: i for i, k in enumerate(PLAN)}
NSLAB = len(PLAN)


def conv_group(key):
    kind, l = key[0], key[1]
    if kind == "kv":
        return 2
    return l


def _b_in(W, cols_list):
    K = W.shape[0] // 128
    out = np.zeros((128, SLAB), np.float32)
    o4 = out[:, : len(cols_list) * K * 128].reshape(128, len(cols_list), K, 128)
    Wr = W.reshape(K, 128, W.shape[1])
    for jj, cols in enumerate(cols_list):
        o4[:, jj] = Wr[:, :, cols].transpose(1, 0, 2)
    return out


def _a_in(W, col0, ncols):
    K = W.shape[0] // 128
    out = np.zeros((128, SLAB), np.float32)
    o3 = out[:, : K * ncols].reshape(128, K, ncols)
    o3[:] = W.reshape(K, 128, W.shape[1])[:, :, col0:col0 + ncols].transpose(1, 0, 2)
    return out


def _b_out(W, dchunks):
    NF = W.shape[0] // 128
    out = np.zeros((128, SLAB), np.float32)
    o4 = out[:, : len(dchunks) * NF * 128].reshape(128, len(dchunks), NF, 128)
    Wr = W.reshape(NF, 128, W.shape[1])
    for ii, dc in enumerate(dchunks):
        o4[:, ii] = Wr[:, :, dc * 128:(dc + 1) * 128].transpose(1, 0, 2)
    return out


def build_slabs(ret_w_in, ret_w_out, kv_w, swa_w_q, swa_w_o, ffn_w_in, ffn_w_out):
    wsl = np.zeros((NSLAB, 128, SLAB), np.float32)
    ar = np.arange(128)
    for i, key in enumerate(PLAN):
        kind, l, what, s = key
        if kind == "ret":
            Wi = ret_w_in[l]
            if what == "q":
                wsl[i] = _b_in(Wi, [(s * 4 + jj) * 128 + ar for jj in range(4)])
            elif what == "k":
                wsl[i] = _b_in(Wi, [1024 + (s * 4 + jj) * 128 + ar for jj in range(4)])
            elif what == "v":
                wsl[i] = _a_in(Wi, 2048 + s * 512, 512)
            elif what == "g":
                wsl[i] = _a_in(Wi, 4096 + s * 512, 512)
            else:
                wsl[i] = _b_out(ret_w_out[l], [2 * s, 2 * s + 1])
        elif kind == "ffn":
            if what == "in":
                cl = []
                for pp in range(2):
                    j = 2 * s + pp
                    cl.append(j * 128 + ar)
                    cl.append(FH + j * 128 + ar)
                wsl[i] = _b_in(ffn_w_in[l], cl)
            else:
                wsl[i] = _b_out(ffn_w_out[l], [s])
        elif kind == "kv":
            if what == "k":
                a64 = np.arange(64)
                wsl[i] = _b_in(kv_w, [np.concatenate([k * 64 + a64, k * 64 + a64]) for k in range(4)])
            else:
                wsl[i] = _a_in(kv_w, 256, 256)
        else:
            b = l - 2
            if what == "q":
                wsl[i] = _b_in(swa_w_q[b], [(s * 4 + jj) * 128 + ar for jj in range(4)])
            else:
                wsl[i] = _b_out(swa_w_o[b], [4 * s + ii for ii in range(4)])
    return wsl


C_IDENT = 0
C_XS = 128
C_KS = C_XS + 512
C_MASK = C_KS + 512
C_NDF = C_MASK + 128
C_NDR = C_NDF + 256
C_NDM = C_NDR + 256
C_GCOL = C_NDM + 512
C_SINK = C_GCOL + 80
NCW = C_SINK + 32


def ret_consts():
    hh = np.arange(RH, dtype=np.float32)
    lg = np.log1p(-np.exp2(-5.0 - hh)).astype(np.float32)
    return lg


def build_consts(mix_norm, ffn_norm, kv_norm, final_norm, swa_sinks):
    c = np.zeros((128, NCW), np.float32)
    c[:, C_IDENT:C_IDENT + 128] = np.eye(128, dtype=np.float32)
    lg = ret_consts()
    i = np.arange(128, dtype=np.float32)
    for h in range(RH):
        xs = (np.float32(DK ** -0.5) * np.exp(lg[h] * (i + 1.0))).astype(np.float32)
        ks = np.exp(-lg[h] * (i + 1.0)).astype(np.float32)
        c[:, C_XS + h * 128:C_XS + (h + 1) * 128] = xs[None, :]
        c[:, C_KS + h * 128:C_KS + (h + 1) * 128] = ks[None, :]
    s_idx = np.arange(128)[:, None]
    c_idx = np.arange(128)[None, :]
    c[:, C_MASK:C_MASK + 128] = (c_idx >= s_idx).astype(np.float32)
    r = np.arange(128)[:, None]
    cc = np.arange(256)[None, :]
    dist = 128 + r - cc
    valid = (dist >= 0) & (dist < 128)
    ndr = np.where(valid, -dist.astype(np.float32), -BIG).astype(np.float32)
    ndf = ndr.copy()
    ndf[:, :128] = -BIG
    c[:, C_NDF:C_NDF + 256] = ndf
    c[:, C_NDR:C_NDR + 256] = ndr
    m = np.arange(16)[None, None, :]
    n = np.arange(32)[None, :, None]
    ndm = -(NMETA + 128 * n + r[:, :, None] - m).astype(np.float32)
    c[:, C_NDM:C_NDM + 512] = ndm.reshape(128, 512)
    gains = np.concatenate([mix_norm, ffn_norm, kv_norm[None], final_norm[None]], axis=0)
    c[:, C_GCOL:C_GCOL + 80] = gains.reshape(10, 8, 128).transpose(2, 0, 1).reshape(128, 80)
    c[:, C_SINK:C_SINK + 32] = np.broadcast_to(swa_sinks.reshape(1, 32), (128, 32))
    return c


CELL = 64
SAME_ENG_SYNC = True
EMBED_WAITS = True
TRANSITIVE = True
SELF_RAW_ONLY = False
LAZY_CONV = True


class Space:
    def __init__(self, nbytes, cell=CELL):
        self.cell = cell
        n = (nbytes + cell - 1) // cell
        self.lw = [None] * n
        self.rd = [None] * n


def _dsz(dt):
    return 2 if dt == BF16 else 4


def cells_of(ap, CELL=CELL):
    dims = ap.ap
    dsz = _dsz(ap.dtype)
    pstep = dims[0][0]
    off = ap.offset % pstep if pstep else ap.offset
    free = list(dims[1:])
    if not free:
        runs = [(off, 1)]
    else:
        lstep, lcnt = free[-1]
        if lstep == 1:
            outer = free[:-1]
            blen = lcnt
        elif lstep == 0:
            outer = free[:-1]
            blen = 1
        else:
            outer = free
            blen = 1
        starts = [off]
        for st, ct in outer:
            if st == 0:
                continue
            starts = [s + k * st for s in starts for k in range(ct)]
            if len(starts) > 512:
                break
        if len(starts) > 512:
            lo = off
            hi = off + sum(st * (ct - 1) for st, ct in free) + 1
            runs = [(lo, hi - lo)]
        else:
            runs = [(s, blen) for s in starts]
    cells = set()
    for s, ln in runs:
        b0 = s * dsz
        b1 = (s + ln) * dsz
        cells.update(range(b0 // CELL, (b1 - 1) // CELL + 1))
    return cells


class Prog:
    ENGS = ("PE", "ACT", "DVE", "POOL", "SP")

    def __init__(self, nc, sb_bytes):
        self.nc = nc
        self.streams = {k: [] for k in self.ENGS}
        self.cnt = {}
        self.waited = {k: {} for k in self.ENGS}
        self.sp = {"SB": Space(sb_bytes), "PSUM": Space(16384, 2048)}
        self.dram = {}
        self.same_eng_sync = SAME_ENG_SYNC
        self.know = {}
        self.ninstr = 0

    def _space(self, ap):
        s = str(ap.space)
        return self.sp["PSUM" if "PSUM" in s else "SB"]

    def emit(self, eng, fn, reads=(), writes=(), dr=(), dw=(), chan=None, embed=False):
        who = chan if chan is not None else eng
        deps = {}
        racc = []
        wacc = []
        for ap in reads:
            sp = self._space(ap)
            cs = cells_of(ap, sp.cell)
            racc.append((sp, cs))
            lwl = sp.lw
            for c in cs:
                lw = lwl[c]
                if lw is not None and deps.get(lw[0], 0) < lw[1]:
                    deps[lw[0]] = lw[1]
        raw_self = deps.get(eng, 0)
        for ap in writes:
            sp = self._space(ap)
            cs = cells_of(ap, sp.cell)
            wacc.append((sp, cs))
            lwl = sp.lw
            rdl = sp.rd
            for c in cs:
                lw = lwl[c]
                if lw is not None and deps.get(lw[0], 0) < lw[1]:
                    deps[lw[0]] = lw[1]
                rd = rdl[c]
                if rd:
                    for p, v in rd.items():
                        if deps.get(p, 0) < v:
                            deps[p] = v
        for k in dr:
            st = self.dram.get(k)
            if st and st["lw"] is not None:
                p, v = st["lw"]
                if deps.get(p, 0) < v:
                    deps[p] = v
        for k in dw:
            st = self.dram.get(k)
            if st:
                if st["lw"] is not None:
                    p, v = st["lw"]
                    if deps.get(p, 0) < v:
                        deps[p] = v
                for p, v in st["rd"].items():
                    if deps.get(p, 0) < v:
                        deps[p] = v
        waits = []
        wd = self.waited[eng]
        if SELF_RAW_ONLY and eng in deps:
            if raw_self:
                deps[eng] = raw_self
            else:
                del deps[eng]
        items = [(p, v) for p, v in deps.items()
                 if not (p == eng and (eng == "PE" or not self.same_eng_sync)) and wd.get(p, 0) < v]
        if TRANSITIVE and len(items) > 1:
            keep = []
            for p, v in items:
                cov = False
                for p2, v2 in items:
                    if (p2, v2) != (p, v):
                        kn = self.know.get((p2, v2))
                        if kn and kn.get(p, 0) >= v:
                            cov = True
                            break
                if not cov:
                    keep.append((p, v))
            items = keep
        for p, v in items:
            if wd.get(p, 0) >= v:
                continue
            wd[p] = v
            waits.append((p, v))
            if TRANSITIVE:
                kn = self.know.get((p, v))
                if kn:
                    for q, u in kn.items():
                        if wd.get(q, 0) < u:
                            wd[q] = u
        inc = 16 if chan is not None else 1
        val = self.cnt.get(who, 0) + inc
        self.cnt[who] = val
        if TRANSITIVE:
            self.know[(who, val)] = dict(wd)
        for sp, cs in racc:
            rdl = sp.rd
            for c in cs:
                rd = rdl[c]
                if rd is None:
                    rdl[c] = {who: val}
                else:
                    rd[who] = val
        for sp, cs in wacc:
            lwl = sp.lw
            rdl = sp.rd
            t = (who, val)
            for c in cs:
                lwl[c] = t
                rdl[c] = None
        for k in dr:
            st = self.dram.setdefault(k, {"lw": None, "rd": {}})
            st["rd"][who] = val
        for k in dw:
            self.dram[k] = {"lw": (who, val), "rd": {}}
        self.streams[eng].append((fn, waits, who, inc, embed and EMBED_WAITS))
        self.ninstr += 1 + len(waits)

    def replay(self, eng, e, sems):
        for fn, waits, who, inc, embed in self.streams[eng]:
            ws = list(waits)
            first = ws.pop() if (embed and ws) else None
            for p, v in ws:
                e.wait_ge(sems[p], v)
            ins = fn(e)
            if first is not None:
                ins._wait_ge(sems[first[0]], first[1])
            ins.then_inc(sems[who], inc)


class _Stop(Exception):
    pass


def build_program(ngroups=9, nlayers=4, debug_points=(), stop_at=None):
    nc = bass.Bass("TRN2", target_bir_lowering=False)
    x_d = nc.dram_tensor("x", [SEQ, D], F32, kind="ExternalInput").ap()
    meta_d = nc.dram_tensor("meta", [NMETA, D], F32, kind="ExternalInput").ap()
    wsl_d = nc.dram_tensor("wsl", [NSLAB, 128, SLAB], F32, kind="ExternalInput").ap()
    cst_d = nc.dram_tensor("cst", [128, NCW], F32, kind="ExternalInput").ap()
    out_d = nc.dram_tensor("out", [SEQ, D], F32, kind="ExternalOutput").ap()
    wsc_d = nc.dram_tensor("wsc", [NSLAB, 128, SLAB], BF16, kind="Internal").ap()
    ndbg = max(1, len(debug_points))
    dbg_d = None
    if debug_points or stop_at is not None:
        dbg_d = nc.dram_tensor("dbg", [ndbg, 128, 4096], F32, kind="ExternalOutput").ap()

    class _A:
        off = 0

    def alloc(nbytes):
        o = (_A.off + 63) // 64 * 64
        _A.off = o + nbytes
        return o

    o_cst = alloc(NCW * 4)
    o_identb = alloc(256)
    o_onesb = alloc(256)
    o_negh = alloc(64)
    o_hT = alloc(8 * 2048)
    o_xnT = alloc(8 * 1024)
    o_sq = alloc(2 * 1024)
    o_rstd = alloc(2048)
    o_S = alloc(2 * 4 * 2 * 2048)
    o_Sbf = alloc(4 * 2 * 1024)
    NRING = 5
    o_ring = alloc(NRING * 8192)
    o_A = alloc(8192)
    o_B = alloc(8192)
    o_C = alloc(16384)
    o_E = alloc(16384)
    o_G = alloc(16384)
    o_sm = alloc(9728)
    o_KB = alloc(4 * 656 * 2)
    o_Vt = alloc(6 * 512)
    o_Vbd = alloc(2048)
    o_pm4 = alloc(2 * 128)
    o_small = alloc(2560)
    SB_BYTES = _A.off
    assert SB_BYTES <= 212000, SB_BYTES

    import contextlib
    es = contextlib.ExitStack()
    with es:
        ar_t = es.enter_context(nc.sbuf_tensor("arena", [128, SB_BYTES // 4 + 16], F32))
        ps_t = es.enter_context(nc.psum_tensor("psarena", [128, 8, 512], F32))
        AR = ar_t[:, :]
        ARB = AR.bitcast(BF16)
        PSF = ps_t[:, :, :].rearrange("p a b -> p (a b)")
        PSB = PSF.bitcast(BF16)

        def f32v(off, n):
            return AR[:, off // 4: off // 4 + n]

        def b16v(off, n):
            return ARB[:, off // 2: off // 2 + n]

        def psf(bank, boff, n):
            s = bank * 512 + boff // 4
            return PSF[:, s:s + n]

        def psb(bank, boff, n):
            s = bank * 1024 + boff // 2
            return PSB[:, s:s + n]

        P = Prog(nc, SB_BYTES + 64)

        cst = f32v(o_cst, NCW)
        ident_f = cst[:, C_IDENT:C_IDENT + 128]
        xs = [cst[:, C_XS + h * 128:C_XS + (h + 1) * 128] for h in range(4)]
        ks = [cst[:, C_KS + h * 128:C_KS + (h + 1) * 128] for h in range(4)]
        maskT = cst[:, C_MASK:C_MASK + 128]
        ndf = cst[:, C_NDF:C_NDF + 256]
        ndr = cst[:, C_NDR:C_NDR + 256]
        ndm = cst[:, C_NDM:C_NDM + 512]
        gcol = cst[:, C_GCOL:C_GCOL + 80]
        sinks = cst[:, C_SINK:C_SINK + 32]
        ident_b = b16v(o_identb, 128)
        ones_b = b16v(o_onesb, 128)
        negh = f32v(o_negh, 1)
        epsr = f32v(o_negh + 4, 1)
        hT = [f32v(o_hT + dc * 2048, 512) for dc in range(8)]
        xnT = [b16v(o_xnT + dc * 1024, 512) for dc in range(8)]
        sq = [b16v(o_sq + i * 1024, 512) for i in range(2)]
        rstd = f32v(o_rstd, 512)
        S = [[[f32v(o_S + ((l * 4 + h) * 2 + dck) * 2048, 512) for dck in range(2)] for h in range(4)] for l in range(2)]
        Sbf = [[b16v(o_Sbf + (h * 2 + dck) * 1024, 512) for dck in range(2)] for h in range(4)]
        ring = [b16v(o_ring + i * 8192, 4096) for i in range(NRING)]
        qT = [b16v(o_A + i * 1024, 512) for i in range(8)]
        slt = [f32v(o_A + i * 2048, 512) for i in range(2)]
        kT = [b16v(o_B + i * 1024, 512) for i in range(8)]
        hid = [b16v(o_C + i * 1024, 512) for i in range(16)] + [b16v(o_B + i * 1024, 512) for i in range(6)]
        gT = [b16v(o_C + i * 1024, 512) for i in range(16)]
        yT = [f32v(o_C + i * 2048, 512) for i in range(8)]
        vv = [[b16v(o_E + (c * 4 + h) * 1024, 512) for h in range(4)] for c in range(4)]
        xin = f32v(o_E, 4096)
        sg = [[b16v(o_G + (c * 4 + h) * 1024, 512) for h in range(4)] for c in range(4)]
        ost = f32v(o_G, 4096)
        otok = [b16v(o_G + i * 2048, 1024) for i in range(2)]
        kz = [[b16v(o_sm + (s * 4 + h) * 512, 256) for h in range(4)] for s in range(2)]
        smT = [b16v(o_sm + 4096 + i * 256, 128) for i in range(2)]
        onb = [b16v(o_sm + 4608 + i * 1024, 512) for i in range(2)]
        gated = [b16v(o_sm + 6656 + i * 1024, 512) for i in range(3)]
        s_sb = [f32v(o_sm + i * 1152, 273) for i in range(3)]
        p_sb = [b16v(o_sm + 3456 + i * 576, 273) for i in range(3)]
        pT = [b16v(o_sm + 5184 + i * 768, 384) for i in range(3)]
        ndw = [f32v(o_sm + 7488 + i * 1088, 272) for i in range(2)]
        KB = [b16v(o_KB + k * 1312, 656) for k in range(4)]
        Vt = [b16v(o_Vt + i * 512, 256) for i in range(6)]
        Vbd = b16v(o_Vbd, 1024)
        pm4 = [b16v(o_pm4 + i * 128, 64) for i in range(2)]
        sm_i = [0]

        def small(n=1):
            i = sm_i[0]
            sm_i[0] = (i + 1) % 28
            return f32v(o_small + i * 64, n)

        invs_t = [f32v(o_small + (28 + i) * 64, 16) for i in range(2)]
        rs16_t = [f32v(o_small + (30 + i) * 64, 16) for i in range(2)]
        es16_t = [f32v(o_small + (32 + i) * 64, 16) for i in range(2)]
        den16_t = [f32v(o_small + (34 + i) * 64, 16) for i in range(2)]

        acc = [psf(0, 0, 512), psf(1, 0, 512)]
        psg = [psf(2, 0, 512), psf(4, 0, 512)]
        psu = [psf(3, 0, 512), psf(5, 0, 512)]
        ps_sc = [psf(i, 0, 128) for i in range(2)]
        kzps = [psb(i, 1024, 256) for i in range(2)]
        ps_o = [psf(2, 0, 512), psf(3, 0, 512)]
        ps_S = [psf(4, 0, 512), psf(5, 0, 512)]
        gtps = [psb(6, 0, 512), psb(7, 0, 512)]
        ps_st = psf(7, 0, 512)
        sw_s = [psf(0, 0, 272), psf(1, 0, 272), psf(2, 0, 272)]
        pTps = [psb(3, 0, 384), psb(4, 0, 384), psb(7, 0, 384)]
        sw_o = [psf(5, 0, 512), psf(6, 0, 512)]
        oTps = [psb(0, 0, 512), psb(1, 0, 512)]

        def mm(out, lhsT, rhs, start, stop, skip=False):
            if skip:
                P.emit("PE", lambda e: e.matmul(out, lhsT=lhsT, rhs=rhs, start=start, stop=stop,
                                                skip_group_check=True), reads=[lhsT, rhs], writes=[out])
            else:
                P.emit("PE", lambda e: e.matmul(out, lhsT=lhsT, rhs=rhs, start=start, stop=stop),
                       reads=[lhsT, rhs], writes=[out])

        def tr(out, in_, ident):
            P.emit("PE", lambda e: e.transpose(out, in_, ident), reads=[in_, ident], writes=[out])

        def act(eng_unused, out, in_, func, bias=None, scale=None, accum=None):
            rd = [in_]
            kw = {}
            if bias is not None:
                kw["bias"] = bias
                if not isinstance(bias, (int, float)):
                    rd.append(bias)
            if scale is not None:
                kw["scale"] = scale
                if not isinstance(scale, (int, float)):
                    rd.append(scale)
            wr = [out]
            if accum is not None:
                kw["accum_out"] = accum
                wr.append(accum)
            P.emit("ACT", lambda e: e.activation(out, in_, func, **kw), reads=rd, writes=wr, embed=(accum is None))

        def tt(eng, out, in0, in1, op):
            P.emit(eng, lambda e: e.tensor_tensor(out, in0, in1, op), reads=[in0, in1], writes=[out], embed=True)

        def ts(eng, out, in0, s1, s2, op0, op1=None):
            rd = [in0] + [s for s in (s1, s2) if s is not None and not isinstance(s, (int, float))]
            if op1 is None:
                P.emit(eng, lambda e: e.tensor_scalar(out, in0, s1, None, op0), reads=rd, writes=[out], embed=True)
            else:
                P.emit(eng, lambda e: e.tensor_scalar(out, in0, s1, s2, op0, op1), reads=rd, writes=[out], embed=True)

        def stt(out, in0, scalar, in1, op0, op1):
            rd = [in0, in1] + ([] if isinstance(scalar, (int, float)) else [scalar])
            P.emit("DVE", lambda e: e.scalar_tensor_tensor(out, in0, scalar, in1, op0, op1), reads=rd, writes=[out],
                   embed=True)

        def cp(eng, out, in_):
            if eng == "ACT":
                P.emit("ACT", lambda e: e.copy(out, in_), reads=[in_], writes=[out], embed=True)
            else:
                P.emit(eng, lambda e: e.tensor_copy(out, in_), reads=[in_], writes=[out], embed=True)

        def mset(eng, ap, val):
            P.emit(eng, lambda e: e.memset(ap, val), writes=[ap])

        def dma(eng, out, in_, chan, reads=(), writes=(), dr=(), dw=()):
            P.emit(eng, lambda e: e.dma_start(out=out, in_=in_), reads=reads, writes=writes, dr=dr, dw=dw, chan=chan)

        ring_i = [0]
        deferred = []
        defer_cnt = [0]

        converted = set()

        def ring_load(key):
            sid = SID[key]
            slot = ring_i[0] % NRING
            ring_i[0] += 1
            dst = ring[slot]
            if LAZY_CONV and sid not in converted:
                converted.add(sid)
                dma("POOL", dst, wsl_d[sid], "rngP%d" % slot, writes=[dst])
                dma("SP", wsc_d[sid], dst, "wst%d" % slot, reads=[dst], dw=[("wsc", sid)])
            else:
                dma("SP", dst, wsc_d[sid], "ring%d" % slot, writes=[dst],
                    dr=[("wsc", sid if LAZY_CONV else conv_group(key))])
            if deferred:
                defer_cnt[0] -= 1
                if defer_cnt[0] <= 0:
                    for f in deferred:
                        f()
                    deferred.clear()
            return dst

        dbg_list = list(debug_points)

        def chk(name):
            if stop_at is not None and name == stop_at:
                src = f32v(o_hT, 4096)
                dma("SP", dbg_d[0], src, "dbg", reads=[src])
                raise _Stop()

        def debug_dump(g, tag):
            if (g, tag) in dbg_list:
                k = dbg_list.index((g, tag))
                src = f32v(o_hT, 4096)
                dma("SP", dbg_d[k], src, "dbg", reads=[src])

        if not LAZY_CONV:
            for i, key in enumerate(PLAN):
                cg = conv_group(key)
                dma("POOL", wsc_d[i], wsl_d[i], "cv%d" % cg, dw=[("wsc", cg)])
        dma("SP", cst, cst_d[:, :], "cst", writes=[cst])
        cp("DVE", ident_b, ident_f)
        mset("DVE", ones_b, 1.0)
        mset("DVE", negh, -0.5)
        mset("DVE", epsr, RMS_EPS)
        mset("POOL", f32v(o_S, 8192), 0.0)
        mset("POOL", f32v(o_hT, 4096), 0.0)
        mset("POOL", b16v(o_KB, 4 * 656), 0.0)
        mset("POOL", b16v(o_Vt, 6 * 256), 0.0)
        mset("POOL", Vbd, 0.0)

        def stats_chunk(dc, T):
            sb = sq[dc % 2]
            act(None, sb[:, :T], hT[dc][:, :T], AF.Square)
            mm(ps_st[:, :T], ones_b, sb[:, :T], dc == 0, dc == 7)

        def hT_done(dc, T):
            if dc >= 1:
                stats_chunk(dc - 1, T)
            if dc == 7:
                stats_chunk(7, T)
                act(None, rstd[:, :T], ps_st[:, :T], AF.Ln, bias=epsr, scale=1.0 / D)
                act(None, rstd[:, :T], rstd[:, :T], AF.Exp, scale=-0.5)

        def norm(gi, T, out_list):
            for dc in range(8):
                stt(out_list[dc][:, :T], hT[dc][:, :T], gcol[:, gi * 8 + dc:gi * 8 + dc + 1], rstd[:, :T],
                    ALU.mult, ALU.mult)

        first4 = [acc[0], acc[1], psf(2, 0, 512), psf(3, 0, 512)]

        def proj4_kouter(slab4, banks, T):
            for kc in range(8):
                for jj in range(4):
                    mm(banks[jj][:, :T], slab4[:, jj, kc, :], xnT[kc][:, :T], kc == 0, kc == 7)

        def resid_add(i, ps, T):
            tt("DVE", hT[i][:, :T], hT[i][:, :T], ps[:, :T], ALU.add)
            hT_done(i, T)

        lg = ret_consts()
        gC = [float(np.exp(np.float32(lg[h]) * np.float32(128.0))) for h in range(4)]
        slopes = [float(2.0 ** (-8.0 * (h + 1) / 16.0)) for h in range(16)]

        def load_input(g, T):
            if g == 0:
                x3 = xin[:, 0:1024]
                mset("DVE", x3, 0.0)
                dst = AR[112:128, o_E // 4: o_E // 4 + 1024]
                dma("SP", dst, meta_d[:, :], "xin", writes=[dst])
                nch = 1
            else:
                nch = 4
            x4 = xin.rearrange("p (c d) -> p c d", c=4)
            for dc in range(8):
                ps = acc[dc % 2]
                for c in range(nch):
                    tr(ps[:, c * 128:(c + 1) * 128], x4[:, c, dc * 128:(dc + 1) * 128], ident_f)
                cp("ACT" if dc % 2 else "DVE", hT[dc][:, :T], ps[:, :T])
                hT_done(dc, T)

        def prefetch_x(g):
            t0 = (g - 1) * 512
            src = x_d[t0:t0 + 512, :].rearrange("(c p) d -> p c d", p=128)
            dst = xin.rearrange("p (c d) -> p c d", c=4)
            dma("SP", dst, src, "xin", writes=[xin])

        def retention(l, g, T):
            NCH = T // 128
            PL = "DVE" if (LAZY_CONV and g <= 1) else "POOL"
            chk("load")
            norm(l, T, xnT)
            chk("norm")
            for which in range(2):
                for s in range(2):
                    slab = ring_load(("ret", l, "qk"[which], s)).rearrange("p (j k c) -> p j k c", j=4, k=8)
                    ko = (which == 0 and s == 0)
                    if ko:
                        proj4_kouter(slab, first4, T)
                    for jj in range(4):
                        j = s * 4 + jj
                        h = j // 2
                        ps = first4[jj] if ko else acc[j % 2]
                        if not ko:
                            for kc in range(8):
                                mm(ps[:, :T], slab[:, jj, kc, :], xnT[kc][:, :T], kc == 0, kc == 7)
                        dst = (qT if which == 0 else kT)[j]
                        sc = (xs if which == 0 else ks)[h]
                        tt("DVE", dst[:, :T].rearrange("p (c t) -> p c t", c=NCH),
                           ps[:, :T].rearrange("p (c t) -> p c t", c=NCH),
                           sc.unsqueeze(1).to_broadcast([128, NCH, 128]), ALU.mult)
            chk("qk")
            for which in range(2):
                for h in range(4):
                    slab = ring_load(("ret", l, "vg"[which], h)).rearrange("p (k c) -> p k c", k=8)
                    for c in range(NCH):
                        ps = acc[c % 2]
                        for kc in range(8):
                            mm(ps[:, :512], xnT[kc][:, c * 128:(c + 1) * 128], slab[:, kc, :], kc == 0, kc == 7)
                        if which == 0:
                            cp("ACT", vv[c][h], ps[:, :512])
                        else:
                            act(None, sg[c][h], ps[:, :512], AF.Silu)
            chk("vg")
            for h in range(4):
                for dck in range(2):
                    cp("ACT", Sbf[h][dck], S[l][h][dck])
            steps = [(c, h) for c in range(NCH) for h in range(4)]

            def stepA1(i, c, h):
                cc = slice(c * 128, (c + 1) * 128)
                par = i % 2
                for dck in range(2):
                    mm(ps_sc[par], kT[h * 2 + dck][:, cc], qT[h * 2 + dck][:, cc], dck == 0, dck == 1)
                for dck in range(2):
                    tr(kzps[par][:, dck * 128:(dck + 1) * 128], kT[h * 2 + dck][:, cc], ident_b)
                tt("DVE", smT[par], ps_sc[par], maskT, ALU.mult)
                ts("DVE", kz[c % 2][h], kzps[par], gC[h], None, ALU.mult)

            def stepA2(i, c, h):
                cc = slice(c * 128, (c + 1) * 128)
                par = i % 2
                mm(ps_o[par], smT[par], vv[c][h], True, False)
                mm(ps_o[par], qT[h * 2][:, cc], Sbf[h][0], False, False)
                mm(ps_o[par], qT[h * 2 + 1][:, cc], Sbf[h][1], False, True)
                for dck in range(2):
                    mm(ps_S[dck], kz[c % 2][h][:, dck * 128:(dck + 1) * 128], vv[c][h], True, True)
                S2 = f32v(o_S + ((l * 4 + h) * 2) * 2048, 1024)
                stt(S2, S2, gC[h], PSF[:, 4 * 512:6 * 512], ALU.mult, ALU.add)
                cp("ACT", b16v(o_Sbf + (h * 2) * 1024, 1024), S2)
                st6 = small(6)
                mv = small(2)
                rs = small(1)
                P.emit("DVE", lambda e, o=st6, i_=ps_o[par]: e.bn_stats(o, i_), reads=[ps_o[par]], writes=[st6], embed=True)
                P.emit("DVE", lambda e, o=mv, i_=st6: e.bn_aggr(o, i_), reads=[st6], writes=[mv], embed=True)
                act(None, rs, mv[:, 1:2], AF.Ln, bias=epsr, scale=1.0)
                act(None, rs, rs, AF.Exp, scale=-0.5)
                nmr = small(1)
                ts(PL, nmr, mv[:, 0:1], rs, -1.0, ALU.mult, ALU.mult)
                act(None, onb[par], ps_o[par], AF.Identity, bias=nmr, scale=rs)
                tt(PL, gated[i % 3], onb[par], sg[c][h], ALU.mult)

            def stepB(i, c, h):
                cc = slice(c * 128, (c + 1) * 128)
                par = i % 2
                for ec in range(4):
                    tr(gtps[par][:, ec * 128:(ec + 1) * 128], gated[i % 3][:, ec * 128:(ec + 1) * 128], ident_b)
                dst = b16v(o_C + h * 4 * 1024, 4 * 512).rearrange("p (e t) -> p e t", e=4)[:, :, cc]
                cp("DVE", dst, gtps[par].rearrange("p (e t) -> p e t", e=4))

            stepA1(0, *steps[0])
            for i, (c, h) in enumerate(steps):
                if i + 1 < len(steps):
                    stepA1(i + 1, *steps[i + 1])
                stepA2(i, c, h)
                if i >= 2:
                    stepB(i - 2, *steps[i - 2])
            for i in range(max(0, len(steps) - 2), len(steps)):
                stepB(i, *steps[i])
            chk("scan")
            for s in range(4):
                slab = ring_load(("ret", l, "o", s)).rearrange("p (i e c) -> p i e c", i=2, e=16)
                for ii in range(2):
                    i = s * 2 + ii
                    ps = acc[i % 2]
                    for ec in range(16):
                        mm(ps[:, :T], slab[:, ii, ec, :], gT[ec][:, :T], ec == 0, ec == 15)
                    resid_add(i, ps, T)

        def ffn(l, g, T):
            chk("mix")
            norm(4 + l, T, xnT)
            for s in range(11):
                slab = ring_load(("ffn", l, "in", s)).rearrange("p (j k c) -> p j k c", j=4, k=8)
                if s == 0:
                    proj4_kouter(slab, [psg[0], psu[0], psg[1], psu[1]], T)
                for pp in range(2):
                    j = 2 * s + pp
                    par = j % 2
                    if s != 0:
                        for kc in range(8):
                            mm(psg[par][:, :T], slab[:, 2 * pp, kc, :], xnT[kc][:, :T], kc == 0, kc == 7)
                        for kc in range(8):
                            mm(psu[par][:, :T], slab[:, 2 * pp + 1, kc, :], xnT[kc][:, :T], kc == 0, kc == 7)
                    act(None, slt[par][:, :T], psg[par][:, :T], AF.Silu)
                    tt("DVE", hid[j][:, :T], slt[par][:, :T], psu[par][:, :T], ALU.mult)
            for i in range(8):
                slab = ring_load(("ffn", l, "out", i))[:, :NFC * 128].rearrange("p (f c) -> p f c", f=NFC)
                ps = acc[i % 2]
                for fc in range(NFC):
                    mm(ps[:, :T], slab[:, fc, :], hid[fc][:, :T], fc == 0, fc == NFC - 1)
                resid_add(i, ps, T)

        def kvproj(g, T):
            norm(8, T, xnT)
            slab = ring_load(("kv", 0, "k", 0)).rearrange("p (j k c) -> p j k c", j=4, k=8)
            proj4_kouter(slab, first4, T)
            for k in range(4):
                ps = first4[k]
                if g == 0:
                    cp("ACT", KB[k][:, 0:16], ps[:, 112:128])
                else:
                    cp("ACT", KB[k][:, 144:656], ps[:, :512])
            slab = ring_load(("kv", 0, "v", 0))[:, :2048].rearrange("p (k c) -> p k c", k=8)
            if g == 0:
                for h4 in range(4):
                    M = 16 * h4 + 16
                    ps = acc[h4 % 2]
                    for kc in range(8):
                        mm(ps[0:M, 0:256], xnT[kc][:, 128 - M:128], slab[:, kc, :], kc == 0, kc == 7)
                    dst = Vbd.rearrange("p (k f) -> p k f", k=4)[0:M, :, h4 * 64:(h4 + 1) * 64]
                    cp("ACT", dst, ps[0:M, 0:256].rearrange("p (k d) -> p k d", k=4))
            else:
                for c in range(4):
                    ps = acc[c % 2]
                    for kc in range(8):
                        mm(ps[:, 0:256], xnT[kc][:, c * 128:(c + 1) * 128], slab[:, kc, :], kc == 0, kc == 7)
                    cp("ACT", Vt[2 + c], ps[:, 0:256])

        def swa(l, g):
            b = l - 2
            T = 512
            norm(l, T, xnT)
            for s in range(2):
                slab = ring_load(("swa", l, "q", s)).rearrange("p (j k c) -> p j k c", j=4, k=8)
                if s == 0:
                    proj4_kouter(slab, first4, T)
                for jj in range(4):
                    j = s * 4 + jj
                    ps = first4[jj] if s == 0 else acc[j % 2]
                    if s != 0:
                        for kc in range(8):
                            mm(ps[:, :T], slab[:, jj, kc, :], xnT[kc][:, :T], kc == 0, kc == 7)
                    act(None, qT[j], ps[:, :T], AF.Copy, scale=0.125)
            for n in range(4):
                nb = (g - 1) * 4 + n
                bp = n % 2
                qc = slice(n * 128, (n + 1) * 128)
                PLs = "DVE" if (LAZY_CONV and g <= 1) else "POOL"
                cp(PLs, ndw[bp][:, 0:16], ndm[:, nb * 16:(nb + 1) * 16])
                cp(PLs, ndw[bp][:, 16:272], ndf if nb == 0 else ndr)
                invs = invs_t[bp]

                def sc(h):
                    kvh = h // 4
                    j = h // 2
                    k3 = h % 3
                    pl = slice((h % 2) * 64, (h % 2) * 64 + 64)
                    mm(sw_s[k3][:, 0:16], qT[j][pl, qc], KB[kvh][pl, 0:16], True, True)
                    mm(sw_s[k3][:, 16:272], qT[j][pl, qc], KB[kvh][pl, 16 + n * 128:16 + n * 128 + 256], True, True)

                def sm(h):
                    k3 = h % 3
                    sk = sinks[:, b * 16 + h:b * 16 + h + 1]
                    cp(PLs, s_sb[k3][:, 272:273], sk)
                    stt(s_sb[k3][:, 0:272], ndw[bp], slopes[h], sw_s[k3], ALU.mult, ALU.add)
                    negm = small(1)
                    rsum = rs16_t[bp][:, h:h + 1]
                    P.emit("DVE", lambda e, o=negm, i=s_sb[k3]: e.tensor_reduce(o, i, AX.X, ALU.max, negate=True),
                           reads=[s_sb[k3]], writes=[negm], embed=True)
                    act(None, p_sb[k3], s_sb[k3], AF.Exp, bias=negm, scale=1.0, accum=rsum)
                    cp("ACT", pm4[(h // 4) % 2][:, (h % 4) * 16:(h % 4) * 16 + 16], p_sb[k3][:, 0:16])

                def pv_tr(h):
                    k3 = h % 3
                    tr(pTps[k3][:, 128:256], p_sb[k3][:, 16:144], ident_b)
                    tr(pTps[k3][:, 256:384], p_sb[k3][:, 144:272], ident_b)
                    cp("DVE", pT[k3][:, 128:384], pTps[k3][:, 128:384])

                def pv_mm(h):
                    k3 = h % 3
                    kvh = h // 4
                    kvc = slice(kvh * 64, kvh * 64 + 64)
                    po = sw_o[h // 8][:, (h % 8) * 64:(h % 8) * 64 + 64]
                    mm(po, pT[k3][:, 128:256], Vt[1 + n][:, kvc], h % 8 == 0, False, skip=True)
                    mm(po, pT[k3][:, 256:384], Vt[2 + n][:, kvc], False, False, skip=True)
                    if pend_meta:
                        pend_meta.pop()()
                    if h % 4 == 3:
                        tr(pTps[k3][0:64, 0:128], pm4[kvh % 2], ident_b)
                        cp("DVE", pT[k3][0:64, 0:128], pTps[k3][0:64, 0:128])
                        pg = sw_o[h // 8][:, (kvh % 2) * 256:(kvh % 2) * 256 + 256]

                        def late(pg=pg, k3=k3, kvh=kvh):
                            mm(pg, pT[k3][0:64, 0:128], Vbd[0:64, kvh * 256:(kvh + 1) * 256], False, True, skip=True)
                        pend_meta.append(late)

                AHEAD = 3
                pend_meta = []
                for h0 in range(AHEAD):
                    sc(h0)
                    sm(h0)
                for h in range(16):
                    pv_tr(h)
                    if h + AHEAD < 16:
                        sc(h + AHEAD)
                        sm(h + AHEAD)
                    pv_mm(h)
                while pend_meta:
                    pend_meta.pop()()
                P.emit("DVE", lambda e, o=invs, i=rs16_t[bp]: e.reciprocal(o, i), reads=[rs16_t[bp]], writes=[invs], embed=True)
                for hb in range(2):
                    tt("DVE", otok[bp][:, hb * 512:(hb + 1) * 512].rearrange("p (h d) -> p h d", h=8),
                       sw_o[hb].rearrange("p (h d) -> p h d", h=8),
                       invs[:, hb * 8:(hb + 1) * 8].unsqueeze(2).to_broadcast([128, 8, 64]), ALU.mult)
                for jq in range(2):
                    for jj in range(4):
                        j = jq * 4 + jj
                        tr(oTps[jq][:, jj * 128:(jj + 1) * 128], otok[bp][:, j * 128:(j + 1) * 128], ident_b)
                    dst = b16v(o_B + jq * 4 * 1024, 4 * 512).rearrange("p (e t) -> p e t", e=4)[:, :, qc]
                    cp("DVE", dst, oTps[jq].rearrange("p (e t) -> p e t", e=4))
            for s in range(2):
                slab = ring_load(("swa", l, "o", s)).rearrange("p (i e c) -> p i e c", i=4, e=8)
                for ii in range(4):
                    i = s * 4 + ii
                    ps = acc[i % 2]
                    for jc in range(8):
                        mm(ps[:, :T], slab[:, ii, jc, :], kT[jc][:, :T], jc == 0, jc == 7)
                    resid_add(i, ps, T)

        def final(g):
            T = 512
            t0 = (g - 1) * 512
            norm(9, T, yT)
            if g + 1 < ngroups:
                load_input(g + 1, 512)
                debug_dump(g + 1, "in")
            o4 = ost.rearrange("p (c d) -> p c d", c=4)
            k = 0
            for c in range(4):
                for half in range(2):
                    ps = acc[k % 2]
                    for q in range(4):
                        dc = half * 4 + q
                        tr(ps[:, q * 128:(q + 1) * 128], yT[dc][:, c * 128:(c + 1) * 128], ident_f)
                    cp("ACT" if k % 2 else "DVE", o4[:, c, half * 512:(half + 1) * 512], ps[:, :512])
                    k += 1
            dst = out_d[t0:t0 + 512, :].rearrange("(c p) d -> p c d", p=128)

            def store():
                dma("SP", dst, o4, "ost", reads=[ost])
            return store

        try:
            for g in range(ngroups):
                T = 128 if g == 0 else 512
                if g == 1:
                    pass
                if g <= 1:
                    load_input(g, T)
                    debug_dump(g, "in")
                for l in range(min(2, nlayers)):
                    retention(l, g, T)
                    debug_dump(g, "mix%d" % l)
                    if l == 1 and g == 0 and ngroups > 1:
                        prefetch_x(1)
                    ffn(l, g, T)
                    debug_dump(g, "ffn%d" % l)
                if nlayers > 2:
                    kvproj(g, T)
                if g == 0:
                    continue
                for l in range(2, nlayers):
                    swa(l, g)
                    debug_dump(g, "mix%d" % l)
                    if l == 3 and g + 1 < ngroups:
                        prefetch_x(g + 1)
                    ffn(l, g, T)
                    debug_dump(g, "ffn%d" % l)
                if nlayers == 4:
                    PLe = "DVE" if (LAZY_CONV and g <= 1) else "POOL"
                    for k in range(4):
                        cp(PLe, KB[k][:, 16:144], KB[k][:, 528:656])
                    cp(PLe, Vt[1], Vt[5])
                st = final(g)
                if g + 1 < ngroups:
                    deferred.append(st)
                    defer_cnt[0] = 3
                else:
                    st()
            for f in deferred:
                f()
            deferred.clear()


        except _Stop:
            pass

        names = set(P.cnt.keys())
        sems = {}
        for nm in sorted(names):
            sems[nm] = es.enter_context(nc.semaphore("s_" + nm))
        block = es.enter_context(nc.Block())
        final_waits = [(nm, P.cnt[nm]) for nm in ("ost", "dbg") if nm in P.cnt]

        @block.sync
        def _(e):
            P.replay("SP", e, sems)
            for nm, v in final_waits:
                e.wait_ge(sems[nm], v)

        @block.tensor
        def _(e):
            P.replay("PE", e, sems)

        @block.scalar
        def _(e):
            P.replay("ACT", e, sems)

        @block.vector
        def _(e):
            P.replay("DVE", e, sems)

        @block.gpsimd
        def _(e):
            P.replay("POOL", e, sems)

    return nc, P


def make_in_maps(inputs):
    wsl = build_slabs(np.asarray(inputs["ret_w_in"], np.float32), np.asarray(inputs["ret_w_out"], np.float32),
                      np.asarray(inputs["kv_w"], np.float32), np.asarray(inputs["swa_w_q"], np.float32),
                      np.asarray(inputs["swa_w_o"], np.float32), np.asarray(inputs["ffn_w_in"], np.float32),
                      np.asarray(inputs["ffn_w_out"], np.float32))
    cst = build_consts(np.asarray(inputs["mix_norm"], np.float32), np.asarray(inputs["ffn_norm"], np.float32),
                       np.asarray(inputs["kv_norm"], np.float32), np.asarray(inputs["final_norm"], np.float32),
                       np.asarray(inputs["swa_sinks"], np.float32))
    x = np.asarray(inputs["x"], np.float32)
    meta = np.ascontiguousarray(np.asarray(inputs["meta_tokens"], np.float32))
    return [{"x": np.ascontiguousarray(x[b]), "meta": meta, "wsl": wsl, "cst": cst} for b in range(x.shape[0])]


def kernel(**inputs):
    in_maps = make_in_maps(inputs)
    nc, _ = build_program()
    res = run_bass_kernel_spmd(nc, in_maps, core_ids=list(range(8)))
    return np.stack([np.asarray(r["out"], np.float32) for r in res.results], axis=0)
```

```python
import numpy as np
import concourse.bass as bass
import concourse.mybir as mybir
from concourse.bass_utils import run_bass_kernel_spmd

F32 = mybir.dt.float32
BF16 = mybir.dt.bfloat16
AF = mybir.ActivationFunctionType
ALU = mybir.AluOpType
AX = mybir.AxisListType

D = 1024
SEQ = 4096
NMETA = 16
RH, DK, DV = 4, 256, 512
FH = 2816
NFC = FH // 128
RMS_EPS = 1e-6
GN_EPS = 1e-6
BIG = 1.0e30
SLAB = 4096

def slab_plan():
    plan = []
    for l in range(2):
        for s in range(2):
            plan.append(("ret", l, "q", s))
        for s in range(2):
            plan.append(("ret", l, "k", s))
        for h in range(4):
            plan.append(("ret", l, "v", h))
        for h in range(4):
            plan.append(("ret", l, "g", h))
        for s in range(4):
            plan.append(("ret", l, "o", s))
        for s in range(11):
            plan.append(("ffn", l, "in", s))
        for s in range(8):
            plan.append(("ffn", l, "out", s))
    plan.append(("kv", 0, "k", 0))
    plan.append(("kv", 0, "v", 0))
    for l in range(2, 4):
        for s in range(2):
            plan.append(("swa", l, "q", s))
        for s in range(2):
            plan.append(("swa", l, "o", s))
        for s in range(11):
            plan.append(("ffn", l, "in", s))
        for s in range(8):
            plan.append(("ffn", l, "out", s))
    return plan


PLAN = slab_plan()
SID = {k: i for i, k in enumerate(PLAN)}
NSLAB = len(PLAN)


def conv_group(key):
    kind, l = key[0], key[1]
    if kind == "kv":
        return 2
    return l


def _b_in(W, cols_list):
    K = W.shape[0] // 128
    out = np.zeros((128, SLAB), np.float32)
    o4 = out[:, : len(cols_list) * K * 128].reshape(128, len(cols_list), K, 128)
    Wr = W.reshape(K, 128, W.shape[1])
    for jj, cols in enumerate(cols_list):
        o4[:, jj] = Wr[:, :, cols].transpose(1, 0, 2)
    return out


def _a_in(W, col0, ncols):
    K = W.shape[0] // 128
    out = np.zeros((128, SLAB), np.float32)
    o3 = out[:, : K * ncols].reshape(128, K, ncols)
    o3[:] = W.reshape(K, 128, W.shape[1])[:, :, col0:col0 + ncols].transpose(1, 0, 2)
    return out


def _b_out(W, dchunks):
    NF = W.shape[0] // 128
    out = np.zeros((128, SLAB), np.float32)
    o4 = out[:, : len(dchunks) * NF * 128].reshape(128, len(dchunks), NF, 128)
    Wr = W.reshape(NF, 128, W.shape[1])
    for ii, dc in enumerate(dchunks):
        o4[:, ii] = Wr[:, :, dc * 128:(dc + 1) * 128].transpose(1, 0, 2)
    return out


def build_slabs(ret_w_in, ret_w_out, kv_w, swa_w_q, swa_w_o, ffn_w_in, ffn_w_out):
    wsl = np.zeros((NSLAB, 128, SLAB), np.float32)
    ar = np.arange(128)
    for i, key in enumerate(PLAN):
        kind, l, what, s = key
        if kind == "ret":
            Wi = ret_w_in[l]
            if what == "q":
                wsl[i] = _b_in(Wi, [(s * 4 + jj) * 128 + ar for jj in range(4)])
            elif what == "k":
                wsl[i] = _b_in(Wi, [1024 + (s * 4 + jj) * 128 + ar for jj in range(4)])
            elif what == "v":
                wsl[i] = _a_in(Wi, 2048 + s * 512, 512)
            elif what == "g":
                wsl[i] = _a_in(Wi, 4096 + s * 512, 512)
            else:
                wsl[i] = _b_out(ret_w_out[l], [2 * s, 2 * s + 1])
        elif kind == "ffn":
            if what == "in":
                cl = []
                for pp in range(2):
                    j = 2 * s + pp
                    cl.append(j * 128 + ar)
                    cl.append(FH + j * 128 + ar)
                wsl[i] = _b_in(ffn_w_in[l], cl)
            else:
                wsl[i] = _b_out(ffn_w_out[l], [s])
        elif kind == "kv":
            if what == "k":
                a64 = np.arange(64)
                wsl[i] = _b_in(kv_w, [np.concatenate([k * 64 + a64, k * 64 + a64]) for k in range(4)])
            else:
                wsl[i] = _a_in(kv_w, 256, 256)
        else:
            b = l - 2
            if what == "q":
                wsl[i] = _b_in(swa_w_q[b], [(s * 4 + jj) * 128 + ar for jj in range(4)])
            else:
                wsl[i] = _b_out(swa_w_o[b], [4 * s + ii for ii in range(4)])
    return wsl


C_IDENT = 0
C_XS = 128
C_KS = C_XS + 512
C_MASK = C_KS + 512
C_NDF = C_MASK + 128
C_NDR = C_NDF + 256
C_NDM = C_NDR + 256
C_GCOL = C_NDM + 512
C_SINK = C_GCOL + 80
NCW = C_SINK + 32


def ret_consts():
    hh = np.arange(RH, dtype=np.float32)
    lg = np.log1p(-np.exp2(-5.0 - hh)).astype(np.float32)
    return lg


def build_consts(mix_norm, ffn_norm, kv_norm, final_norm, swa_sinks):
    c = np.zeros((128, NCW), np.float32)
    c[:, C_IDENT:C_IDENT + 128] = np.eye(128, dtype=np.float32)
    lg = ret_consts()
    i = np.arange(128, dtype=np.float32)
    for h in range(RH):
        xs = (np.float32(DK ** -0.5) * np.exp(lg[h] * (i + 1.0))).astype(np.float32)
        ks = np.exp(-lg[h] * (i + 1.0)).astype(np.float32)
        c[:, C_XS + h * 128:C_XS + (h + 1) * 128] = xs[None, :]
        c[:, C_KS + h * 128:C_KS + (h + 1) * 128] = ks[None, :]
    s_idx = np.arange(128)[:, None]
    c_idx = np.arange(128)[None, :]
    c[:, C_MASK:C_MASK + 128] = (c_idx >= s_idx).astype(np.float32)
    r = np.arange(128)[:, None]
    cc = np.arange(256)[None, :]
    dist = 128 + r - cc
    valid = (dist >= 0) & (dist < 128)
    ndr = np.where(valid, -dist.astype(np.float32), -BIG).astype(np.float32)
    ndf = ndr.copy()
    ndf[:, :128] = -BIG
    c[:, C_NDF:C_NDF + 256] = ndf
    c[:, C_NDR:C_NDR + 256] = ndr
    m = np.arange(16)[None, None, :]
    n = np.arange(32)[None, :, None]
    ndm = -(NMETA + 128 * n + r[:, :, None] - m).astype(np.float32)
    c[:, C_NDM:C_NDM + 512] = ndm.reshape(128, 512)
    gains = np.concatenate([mix_norm, ffn_norm, kv_norm[None], final_norm[None]], axis=0)
    c[:, C_GCOL:C_GCOL + 80] = gains.reshape(10, 8, 128).transpose(2, 0, 1).reshape(128, 80)
    c[:, C_SINK:C_SINK + 32] = np.broadcast_to(swa_sinks.reshape(1, 32), (128, 32))
    return c


CELL = 64
SAME_ENG_SYNC = True
EMBED_WAITS = True
TRANSITIVE = True
PE_EMBED = True
LAZY_CONV = True


class Space:
    def __init__(self, nbytes, cell=CELL):
        self.cell = cell
        n = (nbytes + cell - 1) // cell
        self.lw = [None] * n
        self.rd = [None] * n


def _dsz(dt):
    return 2 if dt == BF16 else 4


def cells_of(ap, CELL=CELL):
    dims = ap.ap
    dsz = _dsz(ap.dtype)
    pstep = dims[0][0]
    off = ap.offset % pstep if pstep else ap.offset
    free = list(dims[1:])
    if not free:
        runs = [(off, 1)]
    else:
        lstep, lcnt = free[-1]
        if lstep == 1:
            outer = free[:-1]
            blen = lcnt
        elif lstep == 0:
            outer = free[:-1]
            blen = 1
        else:
            outer = free
            blen = 1
        starts = [off]
        for st, ct in outer:
            if st == 0:
                continue
            starts = [s + k * st for s in starts for k in range(ct)]
            if len(starts) > 512:
                break
        if len(starts) > 512:
            lo = off
            hi = off + sum(st * (ct - 1) for st, ct in free) + 1
            runs = [(lo, hi - lo)]
        else:
            runs = [(s, blen) for s in starts]
    cells = set()
    for s, ln in runs:
        b0 = s * dsz
        b1 = (s + ln) * dsz
        cells.update(range(b0 // CELL, (b1 - 1) // CELL + 1))
    return cells


class Prog:
    ENGS = ("PE", "ACT", "DVE", "POOL", "SP")

    def __init__(self, nc, sb_bytes):
        self.nc = nc
        self.streams = {k: [] for k in self.ENGS}
        self.cnt = {}
        self.waited = {k: {} for k in self.ENGS}
        self.sp = {"SB": Space(sb_bytes), "PSUM": Space(16384, 2048)}
        self.dram = {}
        self.same_eng_sync = SAME_ENG_SYNC
        self.know = {}
        self.ninstr = 0

    def _space(self, ap):
        s = str(ap.space)
        return self.sp["PSUM" if "PSUM" in s else "SB"]

    def emit(self, eng, fn, reads=(), writes=(), dr=(), dw=(), chan=None, embed=False):
        who = chan if chan is not None else eng
        deps = {}
        racc = []
        wacc = []
        for ap in reads:
            sp = self._space(ap)
            cs = cells_of(ap, sp.cell)
            racc.append((sp, cs))
            lwl = sp.lw
            for c in cs:
                lw = lwl[c]
                if lw is not None and deps.get(lw[0], 0) < lw[1]:
                    deps[lw[0]] = lw[1]
        for ap in writes:
            sp = self._space(ap)
            cs = cells_of(ap, sp.cell)
            wacc.append((sp, cs))
            lwl = sp.lw
            rdl = sp.rd
            for c in cs:
                lw = lwl[c]
                if lw is not None and deps.get(lw[0], 0) < lw[1]:
                    deps[lw[0]] = lw[1]
                rd = rdl[c]
                if rd:
                    for p, v in rd.items():
                        if deps.get(p, 0) < v:
                            deps[p] = v
        for k in dr:
            st = self.dram.get(k)
            if st and st["lw"] is not None:
                p, v = st["lw"]
                if deps.get(p, 0) < v:
                    deps[p] = v
        for k in dw:
            st = self.dram.get(k)
            if st:
                if st["lw"] is not None:
                    p, v = st["lw"]
                    if deps.get(p, 0) < v:
                        deps[p] = v
                for p, v in st["rd"].items():
                    if deps.get(p, 0) < v:
                        deps[p] = v
        waits = []
        wd = self.waited[eng]
        for p, v in deps.items():
            if p == eng and (eng == "PE" or not self.same_eng_sync):
                continue
            if wd.get(p, 0) >= v:
                continue
            wd[p] = v
            waits.append((p, v))
            if TRANSITIVE:
                kn = self.know.get((p, v))
                if kn:
                    for q, u in kn.items():
                        if wd.get(q, 0) < u:
                            wd[q] = u
        inc = 16 if chan is not None else 1
        val = self.cnt.get(who, 0) + inc
        self.cnt[who] = val
        if TRANSITIVE:
            self.know[(who, val)] = dict(wd)
        for sp, cs in racc:
            rdl = sp.rd
            for c in cs:
                rd = rdl[c]
                if rd is None:
                    rdl[c] = {who: val}
                else:
                    rd[who] = val
        for sp, cs in wacc:
            lwl = sp.lw
            rdl = sp.rd
            t = (who, val)
            for c in cs:
                lwl[c] = t
                rdl[c] = None
        for k in dr:
            st = self.dram.setdefault(k, {"lw": None, "rd": {}})
            st["rd"][who] = val
        for k in dw:
            self.dram[k] = {"lw": (who, val), "rd": {}}
        self.streams[eng].append((fn, waits, who, inc, embed and EMBED_WAITS))
        self.ninstr += 1 + len(waits)

    def replay(self, eng, e, sems):
        for fn, waits, who, inc, embed in self.streams[eng]:
            ws = list(waits)
            first = ws.pop() if (embed and ws) else None
            for p, v in ws:
                e.wait_ge(sems[p], v)
            ins = fn(e)
            if first is not None:
                ins._wait_ge(sems[first[0]], first[1])
            ins.then_inc(sems[who], inc)


class _Stop(Exception):
    pass


def build_program(ngroups=9, nlayers=4, debug_points=(), stop_at=None):
    nc = bass.Bass("TRN2", target_bir_lowering=False)
    x_d = nc.dram_tensor("x", [SEQ, D], F32, kind="ExternalInput").ap()
    meta_d = nc.dram_tensor("meta", [NMETA, D], F32, kind="ExternalInput").ap()
    wsl_d = nc.dram_tensor("wsl", [NSLAB, 128, SLAB], F32, kind="ExternalInput").ap()
    cst_d = nc.dram_tensor("cst", [128, NCW], F32, kind="ExternalInput").ap()
    out_d = nc.dram_tensor("out", [SEQ, D], F32, kind="ExternalOutput").ap()
    wsc_d = nc.dram_tensor("wsc", [NSLAB, 128, SLAB], BF16, kind="Internal").ap()
    ndbg = max(1, len(debug_points))
    dbg_d = None
    if debug_points or stop_at is not None:
        dbg_d = nc.dram_tensor("dbg", [ndbg, 128, 4096], F32, kind="ExternalOutput").ap()

    class _A:
        off = 0

    def alloc(nbytes):
        o = (_A.off + 63) // 64 * 64
        _A.off = o + nbytes
        return o

    o_cst = alloc(NCW * 4)
    o_identb = alloc(256)
    o_onesb = alloc(256)
    o_negh = alloc(64)
    o_hT = alloc(8 * 2048)
    o_xnT = alloc(8 * 1024)
    o_sq = alloc(2 * 1024)
    o_rstd = alloc(2048)
    o_S = alloc(2 * 4 * 2 * 2048)
    o_Sbf = alloc(4 * 2 * 1024)
    NRING = 5
    o_ring = alloc(NRING * 8192)
    o_A = alloc(8192)
    o_B = alloc(8192)
    o_C = alloc(16384)
    o_E = alloc(16384)
    o_G = alloc(16384)
    o_sm = alloc(9728)
    o_KB = alloc(4 * 656 * 2)
    o_Vt = alloc(6 * 512)
    o_Vbd = alloc(2048)
    o_pm4 = alloc(2 * 128)
    o_small = alloc(2560)
    SB_BYTES = _A.off
    assert SB_BYTES <= 212000, SB_BYTES

    import contextlib
    es = contextlib.ExitStack()
    with es:
        ar_t = es.enter_context(nc.sbuf_tensor("arena", [128, SB_BYTES // 4 + 16], F32))
        ps_t = es.enter_context(nc.psum_tensor("psarena", [128, 8, 512], F32))
        AR = ar_t[:, :]
        ARB = AR.bitcast(BF16)
        PSF = ps_t[:, :, :].rearrange("p a b -> p (a b)")
        PSB = PSF.bitcast(BF16)

        def f32v(off, n):
            return AR[:, off // 4: off // 4 + n]

        def b16v(off, n):
            return ARB[:, off // 2: off // 2 + n]

        def psf(bank, boff, n):
            s = bank * 512 + boff // 4
            return PSF[:, s:s + n]

        def psb(bank, boff, n):
            s = bank * 1024 + boff // 2
            return PSB[:, s:s + n]

        P = Prog(nc, SB_BYTES + 64)

        cst = f32v(o_cst, NCW)
        ident_f = cst[:, C_IDENT:C_IDENT + 128]
        xs = [cst[:, C_XS + h * 128:C_XS + (h + 1) * 128] for h in range(4)]
        ks = [cst[:, C_KS + h * 128:C_KS + (h + 1) * 128] for h in range(4)]
        maskT = cst[:, C_MASK:C_MASK + 128]
        ndf = cst[:, C_NDF:C_NDF + 256]
        ndr = cst[:, C_NDR:C_NDR + 256]
        ndm = cst[:, C_NDM:C_NDM + 512]
        gcol = cst[:, C_GCOL:C_GCOL + 80]
        sinks = cst[:, C_SINK:C_SINK + 32]
        ident_b = b16v(o_identb, 128)
        ones_b = b16v(o_onesb, 128)
        negh = f32v(o_negh, 1)
        epsr = f32v(o_negh + 4, 1)
        hT = [f32v(o_hT + dc * 2048, 512) for dc in range(8)]
        xnT = [b16v(o_xnT + dc * 1024, 512) for dc in range(8)]
        sq = [b16v(o_sq + i * 1024, 512) for i in range(2)]
        rstd = f32v(o_rstd, 512)
        S = [[[f32v(o_S + ((l * 4 + h) * 2 + dck) * 2048, 512) for dck in range(2)] for h in range(4)] for l in range(2)]
        Sbf = [[b16v(o_Sbf + (h * 2 + dck) * 1024, 512) for dck in range(2)] for h in range(4)]
        ring = [b16v(o_ring + i * 8192, 4096) for i in range(NRING)]
        qT = [b16v(o_A + i * 1024, 512) for i in range(8)]
        slt = [f32v(o_A + i * 2048, 512) for i in range(2)]
        kT = [b16v(o_B + i * 1024, 512) for i in range(8)]
        hid = [b16v(o_C + i * 1024, 512) for i in range(16)] + [b16v(o_B + i * 1024, 512) for i in range(6)]
        gT = [b16v(o_C + i * 1024, 512) for i in range(16)]
        yT = [f32v(o_C + i * 2048, 512) for i in range(8)]
        vv = [[b16v(o_E + (c * 4 + h) * 1024, 512) for h in range(4)] for c in range(4)]
        xin = f32v(o_E, 4096)
        sg = [[b16v(o_G + (c * 4 + h) * 1024, 512) for h in range(4)] for c in range(4)]
        ost = f32v(o_G, 4096)
        otok = [b16v(o_G + i * 2048, 1024) for i in range(2)]
        kz = [[b16v(o_sm + (s * 4 + h) * 512, 256) for h in range(4)] for s in range(2)]
        smT = [b16v(o_sm + 4096 + i * 256, 128) for i in range(2)]
        onb = [b16v(o_sm + 4608 + i * 1024, 512) for i in range(2)]
        gated = [b16v(o_sm + 6656 + i * 1024, 512) for i in range(3)]
        s_sb = [f32v(o_sm + i * 1152, 273) for i in range(3)]
        p_sb = [b16v(o_sm + 3456 + i * 576, 273) for i in range(3)]
        pT = [b16v(o_sm + 5184 + i * 768, 384) for i in range(3)]
        ndw = [f32v(o_sm + 7488 + i * 1088, 272) for i in range(2)]
        KB = [b16v(o_KB + k * 1312, 656) for k in range(4)]
        Vt = [b16v(o_Vt + i * 512, 256) for i in range(6)]
        Vbd = b16v(o_Vbd, 1024)
        pm4 = [b16v(o_pm4 + i * 128, 64) for i in range(2)]
        sm_i = [0]

        def small(n=1):
            i = sm_i[0]
            sm_i[0] = (i + 1) % 28
            return f32v(o_small + i * 64, n)

        invs_t = [f32v(o_small + (28 + i) * 64, 16) for i in range(2)]
        rs16_t = [f32v(o_small + (30 + i) * 64, 16) for i in range(2)]
        es16_t = [f32v(o_small + (32 + i) * 64, 16) for i in range(2)]
        den16_t = [f32v(o_small + (34 + i) * 64, 16) for i in range(2)]

        acc = [psf(0, 0, 512), psf(1, 0, 512)]
        psg = [psf(2, 0, 512), psf(4, 0, 512)]
        psu = [psf(3, 0, 512), psf(5, 0, 512)]
        ps_sc = [psf(i, 0, 128) for i in range(2)]
        kzps = [psb(i, 1024, 256) for i in range(2)]
        ps_o = [psf(2, 0, 512), psf(3, 0, 512)]
        ps_S = [psf(4, 0, 512), psf(5, 0, 512)]
        gtps = [psb(6, 0, 512), psb(7, 0, 512)]
        ps_st = psf(7, 0, 512)
        sw_s = [psf(0, 0, 272), psf(1, 0, 272), psf(2, 0, 272)]
        pTps = [psb(3, 0, 384), psb(4, 0, 384), psb(7, 0, 384)]
        sw_o = [psf(5, 0, 512), psf(6, 0, 512)]
        oTps = [psb(0, 0, 512), psb(1, 0, 512)]

        def mm(out, lhsT, rhs, start, stop, skip=False):
            if skip:
                P.emit("PE", lambda e: e.matmul(out, lhsT=lhsT, rhs=rhs, start=start, stop=stop,
                                                skip_group_check=True), reads=[lhsT, rhs], writes=[out], embed=PE_EMBED)
            else:
                P.emit("PE", lambda e: e.matmul(out, lhsT=lhsT, rhs=rhs, start=start, stop=stop),
                       reads=[lhsT, rhs], writes=[out], embed=PE_EMBED)

        def tr(out, in_, ident):
            P.emit("PE", lambda e: e.transpose(out, in_, ident), reads=[in_, ident], writes=[out], embed=PE_EMBED)

        def act(eng_unused, out, in_, func, bias=None, scale=None, accum=None):
            rd = [in_]
            kw = {}
            if bias is not None:
                kw["bias"] = bias
                if not isinstance(bias, (int, float)):
                    rd.append(bias)
            if scale is not None:
                kw["scale"] = scale
                if not isinstance(scale, (int, float)):
                    rd.append(scale)
            wr = [out]
            if accum is not None:
                kw["accum_out"] = accum
                wr.append(accum)
            P.emit("ACT", lambda e: e.activation(out, in_, func, **kw), reads=rd, writes=wr, embed=(accum is None))

        def tt(eng, out, in0, in1, op):
            P.emit(eng, lambda e: e.tensor_tensor(out, in0, in1, op), reads=[in0, in1], writes=[out], embed=True)

        def ts(eng, out, in0, s1, s2, op0, op1=None):
            rd = [in0] + [s for s in (s1, s2) if s is not None and not isinstance(s, (int, float))]
            if op1 is None:
                P.emit(eng, lambda e: e.tensor_scalar(out, in0, s1, None, op0), reads=rd, writes=[out], embed=True)
            else:
                P.emit(eng, lambda e: e.tensor_scalar(out, in0, s1, s2, op0, op1), reads=rd, writes=[out], embed=True)

        def stt(out, in0, scalar, in1, op0, op1):
            rd = [in0, in1] + ([] if isinstance(scalar, (int, float)) else [scalar])
            P.emit("DVE", lambda e: e.scalar_tensor_tensor(out, in0, scalar, in1, op0, op1), reads=rd, writes=[out],
                   embed=True)

        def cp(eng, out, in_):
            if eng == "ACT":
                P.emit("ACT", lambda e: e.copy(out, in_), reads=[in_], writes=[out], embed=True)
            else:
                P.emit(eng, lambda e: e.tensor_copy(out, in_), reads=[in_], writes=[out], embed=True)

        def mset(eng, ap, val):
            P.emit(eng, lambda e: e.memset(ap, val), writes=[ap])

        def dma(eng, out, in_, chan, reads=(), writes=(), dr=(), dw=()):
            P.emit(eng, lambda e: e.dma_start(out=out, in_=in_), reads=reads, writes=writes, dr=dr, dw=dw, chan=chan)

        ring_i = [0]
        deferred = []
        defer_cnt = [0]

        converted = set()

        def ring_load(key):
            sid = SID[key]
            slot = ring_i[0] % NRING
            ring_i[0] += 1
            dst = ring[slot]
            if LAZY_CONV and sid not in converted:
                converted.add(sid)
                dma("POOL", dst, wsl_d[sid], "rngP%d" % slot, writes=[dst])
                dma("SP", wsc_d[sid], dst, "wst%d" % slot, reads=[dst], dw=[("wsc", sid)])
            else:
                dma("SP", dst, wsc_d[sid], "ring%d" % slot, writes=[dst],
                    dr=[("wsc", sid if LAZY_CONV else conv_group(key))])
            if deferred:
                defer_cnt[0] -= 1
                if defer_cnt[0] <= 0:
                    for f in deferred:
                        f()
                    deferred.clear()
            return dst

        dbg_list = list(debug_points)

        def chk(name):
            if stop_at is not None and name == stop_at:
                src = f32v(o_hT, 4096)
                dma("SP", dbg_d[0], src, "dbg", reads=[src])
                raise _Stop()

        def debug_dump(g, tag):
            if (g, tag) in dbg_list:
                k = dbg_list.index((g, tag))
                src = f32v(o_hT, 4096)
                dma("SP", dbg_d[k], src, "dbg", reads=[src])

        if not LAZY_CONV:
            for i, key in enumerate(PLAN):
                cg = conv_group(key)
                dma("POOL", wsc_d[i], wsl_d[i], "cv%d" % cg, dw=[("wsc", cg)])
        dma("SP", cst, cst_d[:, :], "cst", writes=[cst])
        cp("DVE", ident_b, ident_f)
        mset("DVE", ones_b, 1.0)
        mset("DVE", negh, -0.5)
        mset("DVE", epsr, RMS_EPS)
        mset("POOL", f32v(o_S, 8192), 0.0)
        mset("POOL", f32v(o_hT, 4096), 0.0)
        mset("POOL", b16v(o_KB, 4 * 656), 0.0)
        mset("POOL", b16v(o_Vt, 6 * 256), 0.0)
        mset("POOL", Vbd, 0.0)

        def stats_chunk(dc, T):
            sb = sq[dc % 2]
            act(None, sb[:, :T], hT[dc][:, :T], AF.Square)
            mm(ps_st[:, :T], ones_b, sb[:, :T], dc == 0, dc == 7)

        def hT_done(dc, T):
            if dc >= 1:
                stats_chunk(dc - 1, T)
            if dc == 7:
                stats_chunk(7, T)
                act(None, rstd[:, :T], ps_st[:, :T], AF.Ln, bias=epsr, scale=1.0 / D)
                act(None, rstd[:, :T], rstd[:, :T], AF.Exp, scale=-0.5)

        def norm(gi, T, out_list):
            for dc in range(8):
                stt(out_list[dc][:, :T], hT[dc][:, :T], gcol[:, gi * 8 + dc:gi * 8 + dc + 1], rstd[:, :T],
                    ALU.mult, ALU.mult)

        first4 = [acc[0], acc[1], psf(2, 0, 512), psf(3, 0, 512)]

        def proj4_kouter(slab4, banks, T):
            for kc in range(8):
                for jj in range(4):
                    mm(banks[jj][:, :T], slab4[:, jj, kc, :], xnT[kc][:, :T], kc == 0, kc == 7)

        def resid_add(i, ps, T):
            tt("DVE", hT[i][:, :T], hT[i][:, :T], ps[:, :T], ALU.add)
            hT_done(i, T)

        lg = ret_consts()
        gC = [float(np.exp(np.float32(lg[h]) * np.float32(128.0))) for h in range(4)]
        slopes = [float(2.0 ** (-8.0 * (h + 1) / 16.0)) for h in range(16)]

        def load_input(g, T):
            if g == 0:
                x3 = xin[:, 0:1024]
                mset("DVE", x3, 0.0)
                dst = AR[112:128, o_E // 4: o_E // 4 + 1024]
                dma("SP", dst, meta_d[:, :], "xin", writes=[dst])
                nch = 1
            else:
                nch = 4
            x4 = xin.rearrange("p (c d) -> p c d", c=4)
            for dc in range(8):
                ps = acc[dc % 2]
                for c in range(nch):
                    tr(ps[:, c * 128:(c + 1) * 128], x4[:, c, dc * 128:(dc + 1) * 128], ident_f)
                cp("ACT" if dc % 2 else "DVE", hT[dc][:, :T], ps[:, :T])
                hT_done(dc, T)

        def prefetch_x(g):
            t0 = (g - 1) * 512
            src = x_d[t0:t0 + 512, :].rearrange("(c p) d -> p c d", p=128)
            dst = xin.rearrange("p (c d) -> p c d", c=4)
            dma("SP", dst, src, "xin", writes=[xin])

        def retention(l, g, T):
            NCH = T // 128
            PL = "DVE" if (LAZY_CONV and g <= 1) else "POOL"
            chk("load")
            norm(l, T, xnT)
            chk("norm")
            for which in range(2):
                for s in range(2):
                    slab = ring_load(("ret", l, "qk"[which], s)).rearrange("p (j k c) -> p j k c", j=4, k=8)
                    ko = (which == 0 and s == 0)
                    if ko:
                        proj4_kouter(slab, first4, T)
                    for jj in range(4):
                        j = s * 4 + jj
                        h = j // 2
                        ps = first4[jj] if ko else acc[j % 2]
                        if not ko:
                            for kc in range(8):
                                mm(ps[:, :T], slab[:, jj, kc, :], xnT[kc][:, :T], kc == 0, kc == 7)
                        dst = (qT if which == 0 else kT)[j]
                        sc = (xs if which == 0 else ks)[h]
                        tt("DVE", dst[:, :T].rearrange("p (c t) -> p c t", c=NCH),
                           ps[:, :T].rearrange("p (c t) -> p c t", c=NCH),
                           sc.unsqueeze(1).to_broadcast([128, NCH, 128]), ALU.mult)
            chk("qk")
            for which in range(2):
                for h in range(4):
                    slab = ring_load(("ret", l, "vg"[which], h)).rearrange("p (k c) -> p k c", k=8)
                    for c in range(NCH):
                        ps = acc[c % 2]
                        for kc in range(8):
                            mm(ps[:, :512], xnT[kc][:, c * 128:(c + 1) * 128], slab[:, kc, :], kc == 0, kc == 7)
                        if which == 0:
                            cp("ACT", vv[c][h], ps[:, :512])
                        else:
                            act(None, sg[c][h], ps[:, :512], AF.Silu)
            chk("vg")
            for h in range(4):
                for dck in range(2):
                    cp("ACT", Sbf[h][dck], S[l][h][dck])
            steps = [(c, h) for c in range(NCH) for h in range(4)]

            def stepA1(i, c, h):
                cc = slice(c * 128, (c + 1) * 128)
                par = i % 2
                for dck in range(2):
                    mm(ps_sc[par], kT[h * 2 + dck][:, cc], qT[h * 2 + dck][:, cc], dck == 0, dck == 1)
                for dck in range(2):
                    tr(kzps[par][:, dck * 128:(dck + 1) * 128], kT[h * 2 + dck][:, cc], ident_b)
                tt("DVE", smT[par], ps_sc[par], maskT, ALU.mult)
                ts("DVE", kz[c % 2][h], kzps[par], gC[h], None, ALU.mult)

            def stepA2(i, c, h):
                cc = slice(c * 128, (c + 1) * 128)
                par = i % 2
                mm(ps_o[par], smT[par], vv[c][h], True, False)
                mm(ps_o[par], qT[h * 2][:, cc], Sbf[h][0], False, False)
                mm(ps_o[par], qT[h * 2 + 1][:, cc], Sbf[h][1], False, True)
                for dck in range(2):
                    mm(ps_S[dck], kz[c % 2][h][:, dck * 128:(dck + 1) * 128], vv[c][h], True, True)
                S2 = f32v(o_S + ((l * 4 + h) * 2) * 2048, 1024)
                stt(S2, S2, gC[h], PSF[:, 4 * 512:6 * 512], ALU.mult, ALU.add)
                cp("ACT", b16v(o_Sbf + (h * 2) * 1024, 1024), S2)
                st6 = small(6)
                mv = small(2)
                rs = small(1)
                P.emit("DVE", lambda e, o=st6, i_=ps_o[par]: e.bn_stats(o, i_), reads=[ps_o[par]], writes=[st6], embed=True)
                P.emit("DVE", lambda e, o=mv, i_=st6: e.bn_aggr(o, i_), reads=[st6], writes=[mv], embed=True)
                act(None, rs, mv[:, 1:2], AF.Ln, bias=epsr, scale=1.0)
                act(None, rs, rs, AF.Exp, scale=-0.5)
                nmr = small(1)
                ts(PL, nmr, mv[:, 0:1], rs, -1.0, ALU.mult, ALU.mult)
                act(None, onb[par], ps_o[par], AF.Identity, bias=nmr, scale=rs)
                tt(PL, gated[i % 3], onb[par], sg[c][h], ALU.mult)

            def stepB(i, c, h):
                cc = slice(c * 128, (c + 1) * 128)
                par = i % 2
                for ec in range(4):
                    tr(gtps[par][:, ec * 128:(ec + 1) * 128], gated[i % 3][:, ec * 128:(ec + 1) * 128], ident_b)
                dst = b16v(o_C + h * 4 * 1024, 4 * 512).rearrange("p (e t) -> p e t", e=4)[:, :, cc]
                cp("DVE", dst, gtps[par].rearrange("p (e t) -> p e t", e=4))

            stepA1(0, *steps[0])
            for i, (c, h) in enumerate(steps):
                if i + 1 < len(steps):
                    stepA1(i + 1, *steps[i + 1])
                stepA2(i, c, h)
                if i >= 2:
                    stepB(i - 2, *steps[i - 2])
            for i in range(max(0, len(steps) - 2), len(steps)):
                stepB(i, *steps[i])
            chk("scan")
            for s in range(4):
                slab = ring_load(("ret", l, "o", s)).rearrange("p (i e c) -> p i e c", i=2, e=16)
                for ii in range(2):
                    i = s * 2 + ii
                    ps = acc[i % 2]
                    for ec in range(16):
                        mm(ps[:, :T], slab[:, ii, ec, :], gT[ec][:, :T], ec == 0, ec == 15)
                    resid_add(i, ps, T)

        def ffn(l, g, T):
            chk("mix")
            norm(4 + l, T, xnT)
            for s in range(11):
                slab = ring_load(("ffn", l, "in", s)).rearrange("p (j k c) -> p j k c", j=4, k=8)
                if s == 0:
                    proj4_kouter(slab, [psg[0], psu[0], psg[1], psu[1]], T)
                for pp in range(2):
                    j = 2 * s + pp
                    par = j % 2
                    if s != 0:
                        for kc in range(8):
                            mm(psg[par][:, :T], slab[:, 2 * pp, kc, :], xnT[kc][:, :T], kc == 0, kc == 7)
                        for kc in range(8):
                            mm(psu[par][:, :T], slab[:, 2 * pp + 1, kc, :], xnT[kc][:, :T], kc == 0, kc == 7)
                    act(None, slt[par][:, :T], psg[par][:, :T], AF.Silu)
                    tt("DVE", hid[j][:, :T], slt[par][:, :T], psu[par][:, :T], ALU.mult)
            for i in range(8):
                slab = ring_load(("ffn", l, "out", i))[:, :NFC * 128].rearrange("p (f c) -> p f c", f=NFC)
                ps = acc[i % 2]
                for fc in range(NFC):
                    mm(ps[:, :T], slab[:, fc, :], hid[fc][:, :T], fc == 0, fc == NFC - 1)
                resid_add(i, ps, T)

        def kvproj(g, T):
            norm(8, T, xnT)
            slab = ring_load(("kv", 0, "k", 0)).rearrange("p (j k c) -> p j k c", j=4, k=8)
            proj4_kouter(slab, first4, T)
            for k in range(4):
                ps = first4[k]
                if g == 0:
                    cp("ACT", KB[k][:, 0:16], ps[:, 112:128])
                else:
                    cp("ACT", KB[k][:, 144:656], ps[:, :512])
            slab = ring_load(("kv", 0, "v", 0))[:, :2048].rearrange("p (k c) -> p k c", k=8)
            if g == 0:
                for h4 in range(4):
                    M = 16 * h4 + 16
                    ps = acc[h4 % 2]
                    for kc in range(8):
                        mm(ps[0:M, 0:256], xnT[kc][:, 128 - M:128], slab[:, kc, :], kc == 0, kc == 7)
                    dst = Vbd.rearrange("p (k f) -> p k f", k=4)[0:M, :, h4 * 64:(h4 + 1) * 64]
                    cp("ACT", dst, ps[0:M, 0:256].rearrange("p (k d) -> p k d", k=4))
            else:
                for c in range(4):
                    ps = acc[c % 2]
                    for kc in range(8):
                        mm(ps[:, 0:256], xnT[kc][:, c * 128:(c + 1) * 128], slab[:, kc, :], kc == 0, kc == 7)
                    cp("ACT", Vt[2 + c], ps[:, 0:256])

        def swa(l, g):
            b = l - 2
            T = 512
            norm(l, T, xnT)
            for s in range(2):
                slab = ring_load(("swa", l, "q", s)).rearrange("p (j k c) -> p j k c", j=4, k=8)
                if s == 0:
                    proj4_kouter(slab, first4, T)
                for jj in range(4):
                    j = s * 4 + jj
                    ps = first4[jj] if s == 0 else acc[j % 2]
                    if s != 0:
                        for kc in range(8):
                            mm(ps[:, :T], slab[:, jj, kc, :], xnT[kc][:, :T], kc == 0, kc == 7)
                    act(None, qT[j], ps[:, :T], AF.Copy, scale=0.125)
            for n in range(4):
                nb = (g - 1) * 4 + n
                bp = n % 2
                qc = slice(n * 128, (n + 1) * 128)
                PLs = "DVE" if (LAZY_CONV and g <= 1) else "POOL"
                cp(PLs, ndw[bp][:, 0:16], ndm[:, nb * 16:(nb + 1) * 16])
                cp(PLs, ndw[bp][:, 16:272], ndf if nb == 0 else ndr)
                invs = invs_t[bp]

                def sc(h):
                    kvh = h // 4
                    j = h // 2
                    k3 = h % 3
                    pl = slice((h % 2) * 64, (h % 2) * 64 + 64)
                    mm(sw_s[k3][:, 0:16], qT[j][pl, qc], KB[kvh][pl, 0:16], True, True)
                    mm(sw_s[k3][:, 16:272], qT[j][pl, qc], KB[kvh][pl, 16 + n * 128:16 + n * 128 + 256], True, True)

                def sm(h):
                    k3 = h % 3
                    sk = sinks[:, b * 16 + h:b * 16 + h + 1]
                    cp(PLs, s_sb[k3][:, 272:273], sk)
                    stt(s_sb[k3][:, 0:272], ndw[bp], slopes[h], sw_s[k3], ALU.mult, ALU.add)
                    negm = small(1)
                    rsum = rs16_t[bp][:, h:h + 1]
                    P.emit("DVE", lambda e, o=negm, i=s_sb[k3]: e.tensor_reduce(o, i, AX.X, ALU.max, negate=True),
                           reads=[s_sb[k3]], writes=[negm], embed=True)
                    act(None, p_sb[k3], s_sb[k3], AF.Exp, bias=negm, scale=1.0, accum=rsum)
                    cp("ACT", pm4[(h // 4) % 2][:, (h % 4) * 16:(h % 4) * 16 + 16], p_sb[k3][:, 0:16])

                def pv_tr(h):
                    k3 = h % 3
                    tr(pTps[k3][:, 128:256], p_sb[k3][:, 16:144], ident_b)
                    tr(pTps[k3][:, 256:384], p_sb[k3][:, 144:272], ident_b)
                    cp("DVE", pT[k3][:, 128:384], pTps[k3][:, 128:384])

                def pv_mm(h):
                    k3 = h % 3
                    kvh = h // 4
                    kvc = slice(kvh * 64, kvh * 64 + 64)
                    po = sw_o[h // 8][:, (h % 8) * 64:(h % 8) * 64 + 64]
                    mm(po, pT[k3][:, 128:256], Vt[1 + n][:, kvc], h % 8 == 0, False, skip=True)
                    mm(po, pT[k3][:, 256:384], Vt[2 + n][:, kvc], False, False, skip=True)
                    if pend_meta:
                        pend_meta.pop()()
                    if h % 4 == 3:
                        tr(pTps[k3][0:64, 0:128], pm4[kvh % 2], ident_b)
                        cp("DVE", pT[k3][0:64, 0:128], pTps[k3][0:64, 0:128])
                        pg = sw_o[h // 8][:, (kvh % 2) * 256:(kvh % 2) * 256 + 256]

                        def late(pg=pg, k3=k3, kvh=kvh):
                            mm(pg, pT[k3][0:64, 0:128], Vbd[0:64, kvh * 256:(kvh + 1) * 256], False, True, skip=True)
                        pend_meta.append(late)

                AHEAD = 3
                pend_meta = []
                for h0 in range(AHEAD):
                    sc(h0)
                    sm(h0)
                for h in range(16):
                    pv_tr(h)
                    if h + AHEAD < 16:
                        sc(h + AHEAD)
                        sm(h + AHEAD)
                    pv_mm(h)
                while pend_meta:
                    pend_meta.pop()()
                P.emit("DVE", lambda e, o=invs, i=rs16_t[bp]: e.reciprocal(o, i), reads=[rs16_t[bp]], writes=[invs], embed=True)
                for hb in range(2):
                    tt("DVE", otok[bp][:, hb * 512:(hb + 1) * 512].rearrange("p (h d) -> p h d", h=8),
                       sw_o[hb].rearrange("p (h d) -> p h d", h=8),
                       invs[:, hb * 8:(hb + 1) * 8].unsqueeze(2).to_broadcast([128, 8, 64]), ALU.mult)
                for jq in range(2):
                    for jj in range(4):
                        j = jq * 4 + jj
                        tr(oTps[jq][:, jj * 128:(jj + 1) * 128], otok[bp][:, j * 128:(j + 1) * 128], ident_b)
                    dst = b16v(o_B + jq * 4 * 1024, 4 * 512).rearrange("p (e t) -> p e t", e=4)[:, :, qc]
                    cp("DVE", dst, oTps[jq].rearrange("p (e t) -> p e t", e=4))
            for s in range(2):
                slab = ring_load(("swa", l, "o", s)).rearrange("p (i e c) -> p i e c", i=4, e=8)
                for ii in range(4):
                    i = s * 4 + ii
                    ps = acc[i % 2]
                    for jc in range(8):
                        mm(ps[:, :T], slab[:, ii, jc, :], kT[jc][:, :T], jc == 0, jc == 7)
                    resid_add(i, ps, T)

        def final(g):
            T = 512
            t0 = (g - 1) * 512
            norm(9, T, yT)
            if g + 1 < ngroups:
                load_input(g + 1, 512)
                debug_dump(g + 1, "in")
            o4 = ost.rearrange("p (c d) -> p c d", c=4)
            k = 0
            for c in range(4):
                for half in range(2):
                    ps = acc[k % 2]
                    for q in range(4):
                        dc = half * 4 + q
                        tr(ps[:, q * 128:(q + 1) * 128], yT[dc][:, c * 128:(c + 1) * 128], ident_f)
                    cp("ACT" if k % 2 else "DVE", o4[:, c, half * 512:(half + 1) * 512], ps[:, :512])
                    k += 1
            dst = out_d[t0:t0 + 512, :].rearrange("(c p) d -> p c d", p=128)

            def store():
                dma("SP", dst, o4, "ost", reads=[ost])
            return store

        try:
            for g in range(ngroups):
                T = 128 if g == 0 else 512
                if g == 1:
                    pass
                if g <= 1:
                    load_input(g, T)
                    debug_dump(g, "in")
                for l in range(min(2, nlayers)):
                    retention(l, g, T)
                    debug_dump(g, "mix%d" % l)
                    if l == 1 and g == 0 and ngroups > 1:
                        prefetch_x(1)
                    ffn(l, g, T)
                    debug_dump(g, "ffn%d" % l)
                if nlayers > 2:
                    kvproj(g, T)
                if g == 0:
                    continue
                for l in range(2, nlayers):
                    swa(l, g)
                    debug_dump(g, "mix%d" % l)
                    if l == 3 and g + 1 < ngroups:
                        prefetch_x(g + 1)
                    ffn(l, g, T)
                    debug_dump(g, "ffn%d" % l)
                if nlayers == 4:
                    PLe = "DVE" if (LAZY_CONV and g <= 1) else "POOL"
                    for k in range(4):
                        cp(PLe, KB[k][:, 16:144], KB[k][:, 528:656])
                    cp(PLe, Vt[1], Vt[5])
                st = final(g)
                if g + 1 < ngroups:
                    deferred.append(st)
                    defer_cnt[0] = 3
                else:
                    st()
            for f in deferred:
                f()
            deferred.clear()


        except _Stop:
            pass

        names = set(P.cnt.keys())
        sems = {}
        for nm in sorted(names):
            sems[nm] = es.enter_context(nc.semaphore("s_" + nm))
        block = es.enter_context(nc.Block())
        final_waits = [(nm, P.cnt[nm]) for nm in ("ost", "dbg") if nm in P.cnt]

        @block.sync
        def _(e):
            P.replay("SP", e, sems)
            for nm, v in final_waits:
                e.wait_ge(sems[nm], v)

        @block.tensor
        def _(e):
            P.replay("PE", e, sems)

        @block.scalar
        def _(e):
            P.replay("ACT", e, sems)

        @block.vector
        def _(e):
            P.replay("DVE", e, sems)

        @block.gpsimd
        def _(e):
            P.replay("POOL", e, sems)

    return nc, P


def make_in_maps(inputs):
    wsl = build_slabs(np.asarray(inputs["ret_w_in"], np.float32), np.asarray(inputs["ret_w_out"], np.float32),
                      np.asarray(inputs["kv_w"], np.float32), np.asarray(inputs["swa_w_q"], np.float32),
                      np.asarray(inputs["swa_w_o"], np.float32), np.asarray(inputs["ffn_w_in"], np.float32),
                      np.asarray(inputs["ffn_w_out"], np.float32))
    cst = build_consts(np.asarray(inputs["mix_norm"], np.float32), np.asarray(inputs["ffn_norm"], np.float32),
                       np.asarray(inputs["kv_norm"], np.float32), np.asarray(inputs["final_norm"], np.float32),
                       np.asarray(inputs["swa_sinks"], np.float32))
    x = np.asarray(inputs["x"], np.float32)
    meta = np.ascontiguousarray(np.asarray(inputs["meta_tokens"], np.float32))
    return [{"x": np.ascontiguousarray(x[b]), "meta": meta, "wsl": wsl, "cst": cst} for b in range(x.shape[0])]


def kernel(**inputs):
    in_maps = make_in_maps(inputs)
    nc, _ = build_program()
    res = run_bass_kernel_spmd(nc, in_maps, core_ids=list(range(8)))
    return np.stack([np.asarray(r["out"], np.float32) for r in res.results], axis=0)
```

```python
import numpy as np
import concourse.bass as bass
import concourse.mybir as mybir
from concourse.bass_utils import run_bass_kernel_spmd

F32 = mybir.dt.float32
BF16 = mybir.dt.bfloat16
AF = mybir.ActivationFunctionType
ALU = mybir.AluOpType
AX = mybir.AxisListType

D = 1024
SEQ = 4096
NMETA = 16
RH, DK, DV = 4, 256, 512
FH = 2816
NFC = FH // 128
RMS_EPS = 1e-6
GN_EPS = 1e-6
BIG = 1.0e30
SLAB = 4096

def slab_plan():
    plan = []
    for l in range(2):
        for s in range(2):
            plan.append(("ret", l, "q", s))
        for s in range(2):
            plan.append(("ret", l, "k", s))
        for h in range(4):
            plan.append(("ret", l, "v", h))
        for h in range(4):
            plan.append(("ret", l, "g", h))
        for s in range(4):
            plan.append(("ret", l, "o", s))
        for s in range(11):
            plan.append(("ffn", l, "in", s))
        for s in range(8):
            plan.append(("ffn", l, "out", s))
    plan.append(("kv", 0, "k", 0))
    plan.append(("kv", 0, "v", 0))
    for l in range(2, 4):
        for s in range(2):
            plan.append(("swa", l, "q", s))
        for s in range(2):
            plan.append(("swa", l, "o", s))
        for s in range(11):
            plan.append(("ffn", l, "in", s))
        for s in range(8):
            plan.append(("ffn", l, "out", s))
    return plan


PLAN = slab_plan()
SID = {k: i for i, k in enumerate(PLAN)}
NSLAB = len(PLAN)


def conv_group(key):
    kind, l = key[0], key[1]
    if kind == "kv":
        return 2
    return l


def _b_in(W, cols_list):
    K = W.shape[0] // 128
    out = np.zeros((128, SLAB), np.float32)
    o4 = out[:, : len(cols_list) * K * 128].reshape(128, len(cols_list), K, 128)
    Wr = W.reshape(K, 128, W.shape[1])
    for jj, cols in enumerate(cols_list):
        o4[:, jj] = Wr[:, :, cols].transpose(1, 0, 2)
    return out


def _a_in(W, col0, ncols):
    K = W.shape[0] // 128
    out = np.zeros((128, SLAB), np.float32)
    o3 = out[:, : K * ncols].reshape(128, K, ncols)
    o3[:] = W.reshape(K, 128, W.shape[1])[:, :, col0:col0 + ncols].transpose(1, 0, 2)
    return out


def _b_out(W, dchunks):
    NF = W.shape[0] // 128
    out = np.zeros((128, SLAB), np.float32)
    o4 = out[:, : len(dchunks) * NF * 128].reshape(128, len(dchunks), NF, 128)
    Wr = W.reshape(NF, 128, W.shape[1])
    for ii, dc in enumerate(dchunks):
        o4[:, ii] = Wr[:, :, dc * 128:(dc + 1) * 128].transpose(1, 0, 2)
    return out


def build_slabs(ret_w_in, ret_w_out, kv_w, swa_w_q, swa_w_o, ffn_w_in, ffn_w_out):
    wsl = np.zeros((NSLAB, 128, SLAB), np.float32)
    ar = np.arange(128)
    for i, key in enumerate(PLAN):
        kind, l, what, s = key
        if kind == "ret":
            Wi = ret_w_in[l]
            if what == "q":
                wsl[i] = _b_in(Wi, [(s * 4 + jj) * 128 + ar for jj in range(4)])
            elif what == "k":
                wsl[i] = _b_in(Wi, [1024 + (s * 4 + jj) * 128 + ar for jj in range(4)])
            elif what == "v":
                wsl[i] = _a_in(Wi, 2048 + s * 512, 512)
            elif what == "g":
                wsl[i] = _a_in(Wi, 4096 + s * 512, 512)
            else:
                wsl[i] = _b_out(ret_w_out[l], [2 * s, 2 * s + 1])
        elif kind == "ffn":
            if what == "in":
                cl = []
                for pp in range(2):
                    j = 2 * s + pp
                    cl.append(j * 128 + ar)
                    cl.append(FH + j * 128 + ar)
                wsl[i] = _b_in(ffn_w_in[l], cl)
            else:
                wsl[i] = _b_out(ffn_w_out[l], [s])
        elif kind == "kv":
            if what == "k":
                a64 = np.arange(64)
                wsl[i] = _b_in(kv_w, [np.concatenate([k * 64 + a64, k * 64 + a64]) for k in range(4)])
            else:
                wsl[i] = _a_in(kv_w, 256, 256)
        else:
            b = l - 2
            if what == "q":
                wsl[i] = _b_in(swa_w_q[b], [(s * 4 + jj) * 128 + ar for jj in range(4)])
            else:
                wsl[i] = _b_out(swa_w_o[b], [4 * s + ii for ii in range(4)])
    return wsl


C_IDENT = 0
C_XS = 128
C_KS = C_XS + 512
C_MASK = C_KS + 512
C_NDF = C_MASK + 128
C_NDR = C_NDF + 256
C_NDM = C_NDR + 256
C_GCOL = C_NDM + 512
C_SINK = C_GCOL + 80
NCW = C_SINK + 32


def ret_consts():
    hh = np.arange(RH, dtype=np.float32)
    lg = np.log1p(-np.exp2(-5.0 - hh)).astype(np.float32)
    return lg


def build_consts(mix_norm, ffn_norm, kv_norm, final_norm, swa_sinks):
    c = np.zeros((128, NCW), np.float32)
    c[:, C_IDENT:C_IDENT + 128] = np.eye(128, dtype=np.float32)
    lg = ret_consts()
    i = np.arange(128, dtype=np.float32)
    for h in range(RH):
        xs = (np.float32(DK ** -0.5) * np.exp(lg[h] * (i + 1.0))).astype(np.float32)
        ks = np.exp(-lg[h] * (i + 1.0)).astype(np.float32)
        c[:, C_XS + h * 128:C_XS + (h + 1) * 128] = xs[None, :]
        c[:, C_KS + h * 128:C_KS + (h + 1) * 128] = ks[None, :]
    s_idx = np.arange(128)[:, None]
    c_idx = np.arange(128)[None, :]
    c[:, C_MASK:C_MASK + 128] = (c_idx >= s_idx).astype(np.float32)
    r = np.arange(128)[:, None]
    cc = np.arange(256)[None, :]
    dist = 128 + r - cc
    valid = (dist >= 0) & (dist < 128)
    ndr = np.where(valid, -dist.astype(np.float32), -BIG).astype(np.float32)
    ndf = ndr.copy()
    ndf[:, :128] = -BIG
    c[:, C_NDF:C_NDF + 256] = ndf
    c[:, C_NDR:C_NDR + 256] = ndr
    m = np.arange(16)[None, None, :]
    n = np.arange(32)[None, :, None]
    ndm = -(NMETA + 128 * n + r[:, :, None] - m).astype(np.float32)
    c[:, C_NDM:C_NDM + 512] = ndm.reshape(128, 512)
    gains = np.concatenate([mix_norm, ffn_norm, kv_norm[None], final_norm[None]], axis=0)
    c[:, C_GCOL:C_GCOL + 80] = gains.reshape(10, 8, 128).transpose(2, 0, 1).reshape(128, 80)
    c[:, C_SINK:C_SINK + 32] = np.broadcast_to(swa_sinks.reshape(1, 32), (128, 32))
    return c


CELL = 64
SAME_ENG_SYNC = True
EMBED_WAITS = True
TRANSITIVE = True
LAZY_CONV = True


class Space:
    def __init__(self, nbytes, cell=CELL):
        self.cell = cell
        n = (nbytes + cell - 1) // cell
        self.lw = [None] * n
        self.rd = [None] * n


def _dsz(dt):
    return 2 if dt == BF16 else 4


def cells_of(ap, CELL=CELL):
    dims = ap.ap
    dsz = _dsz(ap.dtype)
    pstep = dims[0][0]
    off = ap.offset % pstep if pstep else ap.offset
    free = list(dims[1:])
    if not free:
        runs = [(off, 1)]
    else:
        lstep, lcnt = free[-1]
        if lstep == 1:
            outer = free[:-1]
            blen = lcnt
        elif lstep == 0:
            outer = free[:-1]
            blen = 1
        else:
            outer = free
            blen = 1
        starts = [off]
        for st, ct in outer:
            if st == 0:
                continue
            starts = [s + k * st for s in starts for k in range(ct)]
            if len(starts) > 512:
                break
        if len(starts) > 512:
            lo = off
            hi = off + sum(st * (ct - 1) for st, ct in free) + 1
            runs = [(lo, hi - lo)]
        else:
            runs = [(s, blen) for s in starts]
    cells = set()
    for s, ln in runs:
        b0 = s * dsz
        b1 = (s + ln) * dsz
        cells.update(range(b0 // CELL, (b1 - 1) // CELL + 1))
    return cells


class Prog:
    ENGS = ("PE", "ACT", "DVE", "POOL", "SP")

    def __init__(self, nc, sb_bytes):
        self.nc = nc
        self.streams = {k: [] for k in self.ENGS}
        self.cnt = {}
        self.waited = {k: {} for k in self.ENGS}
        self.sp = {"SB": Space(sb_bytes), "PSUM": Space(16384, 2048)}
        self.dram = {}
        self.same_eng_sync = SAME_ENG_SYNC
        self.know = {}
        self.ninstr = 0

    def _space(self, ap):
        s = str(ap.space)
        return self.sp["PSUM" if "PSUM" in s else "SB"]

    def emit(self, eng, fn, reads=(), writes=(), dr=(), dw=(), chan=None, embed=False):
        who = chan if chan is not None else eng
        deps = {}
        racc = []
        wacc = []
        for ap in reads:
            sp = self._space(ap)
            cs = cells_of(ap, sp.cell)
            racc.append((sp, cs))
            lwl = sp.lw
            for c in cs:
                lw = lwl[c]
                if lw is not None and deps.get(lw[0], 0) < lw[1]:
                    deps[lw[0]] = lw[1]
        for ap in writes:
            sp = self._space(ap)
            cs = cells_of(ap, sp.cell)
            wacc.append((sp, cs))
            lwl = sp.lw
            rdl = sp.rd
            for c in cs:
                lw = lwl[c]
                if lw is not None and deps.get(lw[0], 0) < lw[1]:
                    deps[lw[0]] = lw[1]
                rd = rdl[c]
                if rd:
                    for p, v in rd.items():
                        if deps.get(p, 0) < v:
                            deps[p] = v
        for k in dr:
            st = self.dram.get(k)
            if st and st["lw"] is not None:
                p, v = st["lw"]
                if deps.get(p, 0) < v:
                    deps[p] = v
        for k in dw:
            st = self.dram.get(k)
            if st:
                if st["lw"] is not None:
                    p, v = st["lw"]
                    if deps.get(p, 0) < v:
                        deps[p] = v
                for p, v in st["rd"].items():
                    if deps.get(p, 0) < v:
                        deps[p] = v
        waits = []
        wd = self.waited[eng]
        for p, v in deps.items():
            if p == eng and (eng == "PE" or not self.same_eng_sync):
                continue
            if wd.get(p, 0) >= v:
                continue
            wd[p] = v
            waits.append((p, v))
            if TRANSITIVE:
                kn = self.know.get((p, v))
                if kn:
                    for q, u in kn.items():
                        if wd.get(q, 0) < u:
                            wd[q] = u
        inc = 16 if chan is not None else 1
        val = self.cnt.get(who, 0) + inc
        self.cnt[who] = val
        if TRANSITIVE:
            self.know[(who, val)] = dict(wd)
        for sp, cs in racc:
            rdl = sp.rd
            for c in cs:
                rd = rdl[c]
                if rd is None:
                    rdl[c] = {who: val}
                else:
                    rd[who] = val
        for sp, cs in wacc:
            lwl = sp.lw
            rdl = sp.rd
            t = (who, val)
            for c in cs:
                lwl[c] = t
                rdl[c] = None
        for k in dr:
            st = self.dram.setdefault(k, {"lw": None, "rd": {}})
            st["rd"][who] = val
        for k in dw:
            self.dram[k] = {"lw": (who, val), "rd": {}}
        self.streams[eng].append((fn, waits, who, inc, embed and EMBED_WAITS))
        self.ninstr += 1 + len(waits)

    def replay(self, eng, e, sems):
        for fn, waits, who, inc, embed in self.streams[eng]:
            ws = list(waits)
            first = ws.pop() if (embed and ws) else None
            for p, v in ws:
                e.wait_ge(sems[p], v)
            ins = fn(e)
            if first is not None:
                ins._wait_ge(sems[first[0]], first[1])
            ins.then_inc(sems[who], inc)


class _Stop(Exception):
    pass


def build_program(ngroups=9, nlayers=4, debug_points=(), stop_at=None):
    nc = bass.Bass("TRN2", target_bir_lowering=False)
    x_d = nc.dram_tensor("x", [SEQ, D], F32, kind="ExternalInput").ap()
    meta_d = nc.dram_tensor("meta", [NMETA, D], F32, kind="ExternalInput").ap()
    wsl_d = nc.dram_tensor("wsl", [NSLAB, 128, SLAB], F32, kind="ExternalInput").ap()
    cst_d = nc.dram_tensor("cst", [128, NCW], F32, kind="ExternalInput").ap()
    out_d = nc.dram_tensor("out", [SEQ, D], F32, kind="ExternalOutput").ap()
    wsc_d = nc.dram_tensor("wsc", [NSLAB, 128, SLAB], BF16, kind="Internal").ap()
    ndbg = max(1, len(debug_points))
    dbg_d = None
    if debug_points or stop_at is not None:
        dbg_d = nc.dram_tensor("dbg", [ndbg, 128, 4096], F32, kind="ExternalOutput").ap()

    class _A:
        off = 0

    def alloc(nbytes):
        o = (_A.off + 63) // 64 * 64
        _A.off = o + nbytes
        return o

    o_cst = alloc(NCW * 4)
    o_identb = alloc(256)
    o_onesb = alloc(256)
    o_negh = alloc(64)
    o_hT = alloc(8 * 2048)
    o_xnT = alloc(8 * 1024)
    o_sq = alloc(2 * 1024)
    o_rstd = alloc(2048)
    o_S = alloc(2 * 4 * 2 * 2048)
    o_Sbf = alloc(4 * 2 * 1024)
    NRING = 5
    o_ring = alloc(NRING * 8192)
    o_A = alloc(8192)
    o_B = alloc(8192)
    o_C = alloc(16384)
    o_E = alloc(16384)
    o_G = alloc(16384)
    o_sm = alloc(9728)
    o_KB = alloc(4 * 656 * 2)
    o_Vt = alloc(6 * 512)
    o_Vbd = alloc(2048)
    o_pm4 = alloc(2 * 128)
    o_small = alloc(2560)
    SB_BYTES = _A.off
    assert SB_BYTES <= 212000, SB_BYTES

    import contextlib
    es = contextlib.ExitStack()
    with es:
        ar_t = es.enter_context(nc.sbuf_tensor("arena", [128, SB_BYTES // 4 + 16], F32))
        ps_t = es.enter_context(nc.psum_tensor("psarena", [128, 8, 512], F32))
        AR = ar_t[:, :]
        ARB = AR.bitcast(BF16)
        PSF = ps_t[:, :, :].rearrange("p a b -> p (a b)")
        PSB = PSF.bitcast(BF16)

        def f32v(off, n):
            return AR[:, off // 4: off // 4 + n]

        def b16v(off, n):
            return ARB[:, off // 2: off // 2 + n]

        def psf(bank, boff, n):
            s = bank * 512 + boff // 4
            return PSF[:, s:s + n]

        def psb(bank, boff, n):
            s = bank * 1024 + boff // 2
            return PSB[:, s:s + n]

        P = Prog(nc, SB_BYTES + 64)

        cst = f32v(o_cst, NCW)
        ident_f = cst[:, C_IDENT:C_IDENT + 128]
        xs = [cst[:, C_XS + h * 128:C_XS + (h + 1) * 128] for h in range(4)]
        ks = [cst[:, C_KS + h * 128:C_KS + (h + 1) * 128] for h in range(4)]
        maskT = cst[:, C_MASK:C_MASK + 128]
        ndf = cst[:, C_NDF:C_NDF + 256]
        ndr = cst[:, C_NDR:C_NDR + 256]
        ndm = cst[:, C_NDM:C_NDM + 512]
        gcol = cst[:, C_GCOL:C_GCOL + 80]
        sinks = cst[:, C_SINK:C_SINK + 32]
        ident_b = b16v(o_identb, 128)
        ones_b = b16v(o_onesb, 128)
        negh = f32v(o_negh, 1)
        epsr = f32v(o_negh + 4, 1)
        hT = [f32v(o_hT + dc * 2048, 512) for dc in range(8)]
        xnT = [b16v(o_xnT + dc * 1024, 512) for dc in range(8)]
        sq = [b16v(o_sq + i * 1024, 512) for i in range(2)]
        rstd = f32v(o_rstd, 512)
        S = [[[f32v(o_S + ((l * 4 + h) * 2 + dck) * 2048, 512) for dck in range(2)] for h in range(4)] for l in range(2)]
        Sbf = [[b16v(o_Sbf + (h * 2 + dck) * 1024, 512) for dck in range(2)] for h in range(4)]
        ring = [b16v(o_ring + i * 8192, 4096) for i in range(NRING)]
        qT = [b16v(o_A + i * 1024, 512) for i in range(8)]
        slt = [f32v(o_A + i * 2048, 512) for i in range(2)]
        kT = [b16v(o_B + i * 1024, 512) for i in range(8)]
        hid = [b16v(o_C + i * 1024, 512) for i in range(16)] + [b16v(o_B + i * 1024, 512) for i in range(6)]
        gT = [b16v(o_C + i * 1024, 512) for i in range(16)]
        yT = [f32v(o_C + i * 2048, 512) for i in range(8)]
        vv = [[b16v(o_E + (c * 4 + h) * 1024, 512) for h in range(4)] for c in range(4)]
        xin = f32v(o_E, 4096)
        sg = [[b16v(o_G + (c * 4 + h) * 1024, 512) for h in range(4)] for c in range(4)]
        ost = f32v(o_G, 4096)
        otok = [b16v(o_G + i * 2048, 1024) for i in range(2)]
        kz = [[b16v(o_sm + (s * 4 + h) * 512, 256) for h in range(4)] for s in range(2)]
        smT = [b16v(o_sm + 4096 + i * 256, 128) for i in range(2)]
        onb = [b16v(o_sm + 4608 + i * 1024, 512) for i in range(2)]
        gated = [b16v(o_sm + 6656 + i * 1024, 512) for i in range(3)]
        s_sb = [f32v(o_sm + i * 1152, 273) for i in range(3)]
        p_sb = [b16v(o_sm + 3456 + i * 576, 273) for i in range(3)]
        pT = [b16v(o_sm + 5184 + i * 768, 384) for i in range(3)]
        ndw = [f32v(o_sm + 7488 + i * 1088, 272) for i in range(2)]
        KB = [b16v(o_KB + k * 1312, 656) for k in range(4)]
        Vt = [b16v(o_Vt + i * 512, 256) for i in range(6)]
        Vbd = b16v(o_Vbd, 1024)
        pm4 = [b16v(o_pm4 + i * 128, 64) for i in range(2)]
        sm_i = [0]

        def small(n=1):
            i = sm_i[0]
            sm_i[0] = (i + 1) % 28
            return f32v(o_small + i * 64, n)

        invs_t = [f32v(o_small + (28 + i) * 64, 16) for i in range(2)]
        rs16_t = [f32v(o_small + (30 + i) * 64, 16) for i in range(2)]
        es16_t = [f32v(o_small + (32 + i) * 64, 16) for i in range(2)]
        den16_t = [f32v(o_small + (34 + i) * 64, 16) for i in range(2)]

        acc = [psf(0, 0, 512), psf(1, 0, 512)]
        psg = [psf(2, 0, 512), psf(4, 0, 512)]
        psu = [psf(3, 0, 512), psf(5, 0, 512)]
        ps_sc = [psf(i, 0, 128) for i in range(2)]
        kzps = [psb(i, 1024, 256) for i in range(2)]
        ps_o = [psf(2, 0, 512), psf(3, 0, 512)]
        ps_S = [psf(4, 0, 512), psf(5, 0, 512)]
        gtps = [psb(6, 0, 512), psb(7, 0, 512)]
        ps_st = psf(7, 0, 512)
        sw_s = [psf(0, 0, 272), psf(1, 0, 272), psf(2, 0, 272)]
        pTps = [psb(3, 0, 384), psb(4, 0, 384), psb(7, 0, 384)]
        sw_o = [psf(5, 0, 512), psf(6, 0, 512)]
        oTps = [psb(0, 0, 512), psb(1, 0, 512)]

        def mm(out, lhsT, rhs, start, stop, skip=False):
            if skip:
                P.emit("PE", lambda e: e.matmul(out, lhsT=lhsT, rhs=rhs, start=start, stop=stop,
                                                skip_group_check=True), reads=[lhsT, rhs], writes=[out])
            else:
                P.emit("PE", lambda e: e.matmul(out, lhsT=lhsT, rhs=rhs, start=start, stop=stop),
                       reads=[lhsT, rhs], writes=[out])

        def tr(out, in_, ident):
            P.emit("PE", lambda e: e.transpose(out, in_, ident), reads=[in_, ident], writes=[out])

        def act(eng_unused, out, in_, func, bias=None, scale=None, accum=None):
            rd = [in_]
            kw = {}
            if bias is not None:
                kw["bias"] = bias
                if not isinstance(bias, (int, float)):
                    rd.append(bias)
            if scale is not None:
                kw["scale"] = scale
                if not isinstance(scale, (int, float)):
                    rd.append(scale)
            wr = [out]
            if accum is not None:
                kw["accum_out"] = accum
                wr.append(accum)
            P.emit("ACT", lambda e: e.activation(out, in_, func, **kw), reads=rd, writes=wr, embed=(accum is None))

        def tt(eng, out, in0, in1, op):
            P.emit(eng, lambda e: e.tensor_tensor(out, in0, in1, op), reads=[in0, in1], writes=[out], embed=True)

        def ts(eng, out, in0, s1, s2, op0, op1=None):
            rd = [in0] + [s for s in (s1, s2) if s is not None and not isinstance(s, (int, float))]
            if op1 is None:
                P.emit(eng, lambda e: e.tensor_scalar(out, in0, s1, None, op0), reads=rd, writes=[out], embed=True)
            else:
                P.emit(eng, lambda e: e.tensor_scalar(out, in0, s1, s2, op0, op1), reads=rd, writes=[out], embed=True)

        def stt(out, in0, scalar, in1, op0, op1):
            rd = [in0, in1] + ([] if isinstance(scalar, (int, float)) else [scalar])
            P.emit("DVE", lambda e: e.scalar_tensor_tensor(out, in0, scalar, in1, op0, op1), reads=rd, writes=[out],
                   embed=True)

        def cp(eng, out, in_):
            if eng == "ACT":
                P.emit("ACT", lambda e: e.copy(out, in_), reads=[in_], writes=[out], embed=True)
            else:
                P.emit(eng, lambda e: e.tensor_copy(out, in_), reads=[in_], writes=[out], embed=True)

        def mset(eng, ap, val):
            P.emit(eng, lambda e: e.memset(ap, val), writes=[ap])

        def dma(eng, out, in_, chan, reads=(), writes=(), dr=(), dw=()):
            P.emit(eng, lambda e: e.dma_start(out=out, in_=in_), reads=reads, writes=writes, dr=dr, dw=dw, chan=chan)

        ring_i = [0]
        deferred = []
        defer_cnt = [0]

        converted = set()

        def ring_load(key):
            sid = SID[key]
            slot = ring_i[0] % NRING
            ring_i[0] += 1
            dst = ring[slot]
            if LAZY_CONV and sid not in converted:
                converted.add(sid)
                dma("POOL", dst, wsl_d[sid], "rngP%d" % slot, writes=[dst])
                dma("SP", wsc_d[sid], dst, "wst%d" % slot, reads=[dst], dw=[("wsc", sid)])
            else:
                dma("SP", dst, wsc_d[sid], "ring%d" % slot, writes=[dst],
                    dr=[("wsc", sid if LAZY_CONV else conv_group(key))])
            if deferred:
                defer_cnt[0] -= 1
                if defer_cnt[0] <= 0:
                    for f in deferred:
                        f()
                    deferred.clear()
            return dst

        dbg_list = list(debug_points)

        def chk(name):
            if stop_at is not None and name == stop_at:
                src = f32v(o_hT, 4096)
                dma("SP", dbg_d[0], src, "dbg", reads=[src])
                raise _Stop()

        def debug_dump(g, tag):
            if (g, tag) in dbg_list:
                k = dbg_list.index((g, tag))
                src = f32v(o_hT, 4096)
                dma("SP", dbg_d[k], src, "dbg", reads=[src])

        if not LAZY_CONV:
            for i, key in enumerate(PLAN):
                cg = conv_group(key)
                dma("POOL", wsc_d[i], wsl_d[i], "cv%d" % cg, dw=[("wsc", cg)])
        dma("SP", cst, cst_d[:, :], "cst", writes=[cst])
        cp("DVE", ident_b, ident_f)
        mset("DVE", ones_b, 1.0)
        mset("DVE", negh, -0.5)
        mset("DVE", epsr, RMS_EPS)
        mset("POOL", f32v(o_S, 8192), 0.0)
        mset("POOL", f32v(o_hT, 4096), 0.0)
        mset("POOL", b16v(o_KB, 4 * 656), 0.0)
        mset("POOL", b16v(o_Vt, 6 * 256), 0.0)
        mset("POOL", Vbd, 0.0)

        def stats_chunk(dc, T):
            sb = sq[dc % 2]
            act(None, sb[:, :T], hT[dc][:, :T], AF.Square)
            mm(ps_st[:, :T], ones_b, sb[:, :T], dc == 0, dc == 7)

        def hT_done(dc, T):
            if dc >= 1:
                stats_chunk(dc - 1, T)
            if dc == 7:
                stats_chunk(7, T)
                act(None, rstd[:, :T], ps_st[:, :T], AF.Ln, bias=epsr, scale=1.0 / D)
                act(None, rstd[:, :T], rstd[:, :T], AF.Exp, scale=-0.5)

        def norm(gi, T, out_list):
            for dc in range(8):
                stt(out_list[dc][:, :T], hT[dc][:, :T], gcol[:, gi * 8 + dc:gi * 8 + dc + 1], rstd[:, :T],
                    ALU.mult, ALU.mult)

        first4 = [acc[0], acc[1], psf(2, 0, 512), psf(3, 0, 512)]

        def proj4_kouter(slab4, banks, T):
            for kc in range(8):
                for jj in range(4):
                    mm(banks[jj][:, :T], slab4[:, jj, kc, :], xnT[kc][:, :T], kc == 0, kc == 7)

        def resid_add(i, ps, T):
            tt("DVE", hT[i][:, :T], hT[i][:, :T], ps[:, :T], ALU.add)
            hT_done(i, T)

        lg = ret_consts()
        gC = [float(np.exp(np.float32(lg[h]) * np.float32(128.0))) for h in range(4)]
        slopes = [float(2.0 ** (-8.0 * (h + 1) / 16.0)) for h in range(16)]

        def load_input(g, T):
            if g == 0:
                x3 = xin[:, 0:1024]
                mset("DVE", x3, 0.0)
                dst = AR[112:128, o_E // 4: o_E // 4 + 1024]
                dma("SP", dst, meta_d[:, :], "xin", writes=[dst])
                nch = 1
            else:
                nch = 4
            x4 = xin.rearrange("p (c d) -> p c d", c=4)
            for dc in range(8):
                ps = first4[dc % 4]
                for c in range(nch):
                    tr(ps[:, c * 128:(c + 1) * 128], x4[:, c, dc * 128:(dc + 1) * 128], ident_f)
                cp("ACT" if dc % 2 else "DVE", hT[dc][:, :T], ps[:, :T])
                hT_done(dc, T)

        def prefetch_x(g):
            t0 = (g - 1) * 512
            src = x_d[t0:t0 + 512, :].rearrange("(c p) d -> p c d", p=128)
            dst = xin.rearrange("p (c d) -> p c d", c=4)
            dma("SP", dst, src, "xin", writes=[xin])

        def retention(l, g, T):
            NCH = T // 128
            PL = "DVE" if (LAZY_CONV and g <= 1) else "POOL"
            chk("load")
            norm(l, T, xnT)
            chk("norm")
            for which in range(2):
                for s in range(2):
                    slab = ring_load(("ret", l, "qk"[which], s)).rearrange("p (j k c) -> p j k c", j=4, k=8)
                    ko = (which == 0 and s == 0)
                    if ko:
                        proj4_kouter(slab, first4, T)
                    for jj in range(4):
                        j = s * 4 + jj
                        h = j // 2
                        ps = first4[jj] if ko else acc[j % 2]
                        if not ko:
                            for kc in range(8):
                                mm(ps[:, :T], slab[:, jj, kc, :], xnT[kc][:, :T], kc == 0, kc == 7)
                        dst = (qT if which == 0 else kT)[j]
                        sc = (xs if which == 0 else ks)[h]
                        tt("DVE", dst[:, :T].rearrange("p (c t) -> p c t", c=NCH),
                           ps[:, :T].rearrange("p (c t) -> p c t", c=NCH),
                           sc.unsqueeze(1).to_broadcast([128, NCH, 128]), ALU.mult)
            chk("qk")
            for which in range(2):
                for h in range(4):
                    slab = ring_load(("ret", l, "vg"[which], h)).rearrange("p (k c) -> p k c", k=8)
                    for c in range(NCH):
                        ps = acc[c % 2]
                        for kc in range(8):
                            mm(ps[:, :512], xnT[kc][:, c * 128:(c + 1) * 128], slab[:, kc, :], kc == 0, kc == 7)
                        if which == 0:
                            cp("ACT", vv[c][h], ps[:, :512])
                        else:
                            act(None, sg[c][h], ps[:, :512], AF.Silu)
            chk("vg")
            for h in range(4):
                for dck in range(2):
                    cp("ACT", Sbf[h][dck], S[l][h][dck])
            steps = [(c, h) for c in range(NCH) for h in range(4)]

            def stepA1(i, c, h):
                cc = slice(c * 128, (c + 1) * 128)
                par = i % 2
                for dck in range(2):
                    mm(ps_sc[par], kT[h * 2 + dck][:, cc], qT[h * 2 + dck][:, cc], dck == 0, dck == 1)
                for dck in range(2):
                    tr(kzps[par][:, dck * 128:(dck + 1) * 128], kT[h * 2 + dck][:, cc], ident_b)
                tt("DVE", smT[par], ps_sc[par], maskT, ALU.mult)
                ts("DVE", kz[c % 2][h], kzps[par], gC[h], None, ALU.mult)

            def stepA2(i, c, h):
                cc = slice(c * 128, (c + 1) * 128)
                par = i % 2
                mm(ps_o[par], smT[par], vv[c][h], True, False)
                mm(ps_o[par], qT[h * 2][:, cc], Sbf[h][0], False, False)
                mm(ps_o[par], qT[h * 2 + 1][:, cc], Sbf[h][1], False, True)
                for dck in range(2):
                    mm(ps_S[dck], kz[c % 2][h][:, dck * 128:(dck + 1) * 128], vv[c][h], True, True)
                S2 = f32v(o_S + ((l * 4 + h) * 2) * 2048, 1024)
                stt(S2, S2, gC[h], PSF[:, 4 * 512:6 * 512], ALU.mult, ALU.add)
                cp("ACT", b16v(o_Sbf + (h * 2) * 1024, 1024), S2)
                st6 = small(6)
                mv = small(2)
                rs = small(1)
                P.emit("DVE", lambda e, o=st6, i_=ps_o[par]: e.bn_stats(o, i_), reads=[ps_o[par]], writes=[st6], embed=True)
                P.emit("DVE", lambda e, o=mv, i_=st6: e.bn_aggr(o, i_), reads=[st6], writes=[mv], embed=True)
                act(None, rs, mv[:, 1:2], AF.Ln, bias=epsr, scale=1.0)
                act(None, rs, rs, AF.Exp, scale=-0.5)
                nmr = small(1)
                ts(PL, nmr, mv[:, 0:1], rs, -1.0, ALU.mult, ALU.mult)
                act(None, onb[par], ps_o[par], AF.Identity, bias=nmr, scale=rs)
                tt(PL, gated[i % 3], onb[par], sg[c][h], ALU.mult)

            def stepB(i, c, h):
                cc = slice(c * 128, (c + 1) * 128)
                par = i % 2
                for ec in range(4):
                    tr(gtps[par][:, ec * 128:(ec + 1) * 128], gated[i % 3][:, ec * 128:(ec + 1) * 128], ident_b)
                dst = b16v(o_C + h * 4 * 1024, 4 * 512).rearrange("p (e t) -> p e t", e=4)[:, :, cc]
                cp("DVE", dst, gtps[par].rearrange("p (e t) -> p e t", e=4))

            stepA1(0, *steps[0])
            for i, (c, h) in enumerate(steps):
                if i + 1 < len(steps):
                    stepA1(i + 1, *steps[i + 1])
                stepA2(i, c, h)
                if i >= 2:
                    stepB(i - 2, *steps[i - 2])
            for i in range(max(0, len(steps) - 2), len(steps)):
                stepB(i, *steps[i])
            chk("scan")
            for s in range(4):
                slab = ring_load(("ret", l, "o", s)).rearrange("p (i e c) -> p i e c", i=2, e=16)
                for ii in range(2):
                    i = s * 2 + ii
                    ps = acc[i % 2]
                    for ec in range(16):
                        mm(ps[:, :T], slab[:, ii, ec, :], gT[ec][:, :T], ec == 0, ec == 15)
                    resid_add(i, ps, T)

        def ffn(l, g, T):
            chk("mix")
            norm(4 + l, T, xnT)
            for s in range(11):
                slab = ring_load(("ffn", l, "in", s)).rearrange("p (j k c) -> p j k c", j=4, k=8)
                if s == 0:
                    proj4_kouter(slab, [psg[0], psu[0], psg[1], psu[1]], T)
                for pp in range(2):
                    j = 2 * s + pp
                    par = j % 2
                    if s != 0:
                        for kc in range(8):
                            mm(psg[par][:, :T], slab[:, 2 * pp, kc, :], xnT[kc][:, :T], kc == 0, kc == 7)
                        for kc in range(8):
                            mm(psu[par][:, :T], slab[:, 2 * pp + 1, kc, :], xnT[kc][:, :T], kc == 0, kc == 7)
                    act(None, slt[par][:, :T], psg[par][:, :T], AF.Silu)
                    tt("DVE", hid[j][:, :T], slt[par][:, :T], psu[par][:, :T], ALU.mult)
            for i in range(8):
                slab = ring_load(("ffn", l, "out", i))[:, :NFC * 128].rearrange("p (f c) -> p f c", f=NFC)
                ps = acc[i % 2]
                for fc in range(NFC):
                    mm(ps[:, :T], slab[:, fc, :], hid[fc][:, :T], fc == 0, fc == NFC - 1)
                resid_add(i, ps, T)

        def kvproj(g, T):
            norm(8, T, xnT)
            slab = ring_load(("kv", 0, "k", 0)).rearrange("p (j k c) -> p j k c", j=4, k=8)
            proj4_kouter(slab, first4, T)
            for k in range(4):
                ps = first4[k]
                if g == 0:
                    cp("ACT", KB[k][:, 0:16], ps[:, 112:128])
                else:
                    cp("ACT", KB[k][:, 144:656], ps[:, :512])
            slab = ring_load(("kv", 0, "v", 0))[:, :2048].rearrange("p (k c) -> p k c", k=8)
            if g == 0:
                for h4 in range(4):
                    M = 16 * h4 + 16
                    ps = acc[h4 % 2]
                    for kc in range(8):
                        mm(ps[0:M, 0:256], xnT[kc][:, 128 - M:128], slab[:, kc, :], kc == 0, kc == 7)
                    dst = Vbd.rearrange("p (k f) -> p k f", k=4)[0:M, :, h4 * 64:(h4 + 1) * 64]
                    cp("ACT", dst, ps[0:M, 0:256].rearrange("p (k d) -> p k d", k=4))
            else:
                for c in range(4):
                    ps = acc[c % 2]
                    for kc in range(8):
                        mm(ps[:, 0:256], xnT[kc][:, c * 128:(c + 1) * 128], slab[:, kc, :], kc == 0, kc == 7)
                    cp("ACT", Vt[2 + c], ps[:, 0:256])

        def swa(l, g):
            b = l - 2
            T = 512
            norm(l, T, xnT)
            for s in range(2):
                slab = ring_load(("swa", l, "q", s)).rearrange("p (j k c) -> p j k c", j=4, k=8)
                if s == 0:
                    proj4_kouter(slab, first4, T)
                for jj in range(4):
                    j = s * 4 + jj
                    ps = first4[jj] if s == 0 else acc[j % 2]
                    if s != 0:
                        for kc in range(8):
                            mm(ps[:, :T], slab[:, jj, kc, :], xnT[kc][:, :T], kc == 0, kc == 7)
                    act(None, qT[j], ps[:, :T], AF.Copy, scale=0.125)
            for n in range(4):
                nb = (g - 1) * 4 + n
                bp = n % 2
                qc = slice(n * 128, (n + 1) * 128)
                PLs = "DVE" if (LAZY_CONV and g <= 1) else "POOL"
                cp(PLs, ndw[bp][:, 0:16], ndm[:, nb * 16:(nb + 1) * 16])
                cp(PLs, ndw[bp][:, 16:272], ndf if nb == 0 else ndr)
                invs = invs_t[bp]

                def sc(h):
                    kvh = h // 4
                    j = h // 2
                    k3 = h % 3
                    pl = slice((h % 2) * 64, (h % 2) * 64 + 64)
                    mm(sw_s[k3][:, 0:16], qT[j][pl, qc], KB[kvh][pl, 0:16], True, True)
                    mm(sw_s[k3][:, 16:272], qT[j][pl, qc], KB[kvh][pl, 16 + n * 128:16 + n * 128 + 256], True, True)

                def sm(h):
                    k3 = h % 3
                    sk = sinks[:, b * 16 + h:b * 16 + h + 1]
                    cp(PLs, s_sb[k3][:, 272:273], sk)
                    stt(s_sb[k3][:, 0:272], ndw[bp], slopes[h], sw_s[k3], ALU.mult, ALU.add)
                    negm = small(1)
                    rsum = rs16_t[bp][:, h:h + 1]
                    P.emit("DVE", lambda e, o=negm, i=s_sb[k3]: e.tensor_reduce(o, i, AX.X, ALU.max, negate=True),
                           reads=[s_sb[k3]], writes=[negm], embed=True)
                    act(None, p_sb[k3], s_sb[k3], AF.Exp, bias=negm, scale=1.0, accum=rsum)
                    cp("ACT", pm4[(h // 4) % 2][:, (h % 4) * 16:(h % 4) * 16 + 16], p_sb[k3][:, 0:16])

                def pv_tr(h):
                    k3 = h % 3
                    tr(pTps[k3][:, 128:256], p_sb[k3][:, 16:144], ident_b)
                    tr(pTps[k3][:, 256:384], p_sb[k3][:, 144:272], ident_b)
                    cp("DVE", pT[k3][:, 128:384], pTps[k3][:, 128:384])

                def pv_mm(h):
                    k3 = h % 3
                    kvh = h // 4
                    kvc = slice(kvh * 64, kvh * 64 + 64)
                    po = sw_o[h // 8][:, (h % 8) * 64:(h % 8) * 64 + 64]
                    mm(po, pT[k3][:, 128:256], Vt[1 + n][:, kvc], h % 8 == 0, False, skip=True)
                    mm(po, pT[k3][:, 256:384], Vt[2 + n][:, kvc], False, False, skip=True)
                    if pend_meta:
                        pend_meta.pop()()
                    if h % 4 == 3:
                        tr(pTps[k3][0:64, 0:128], pm4[kvh % 2], ident_b)
                        cp("DVE", pT[k3][0:64, 0:128], pTps[k3][0:64, 0:128])
                        pg = sw_o[h // 8][:, (kvh % 2) * 256:(kvh % 2) * 256 + 256]

                        def late(pg=pg, k3=k3, kvh=kvh):
                            mm(pg, pT[k3][0:64, 0:128], Vbd[0:64, kvh * 256:(kvh + 1) * 256], False, True, skip=True)
                        pend_meta.append(late)

                AHEAD = 3
                pend_meta = []
                for h0 in range(AHEAD):
                    sc(h0)
                    sm(h0)
                for h in range(16):
                    pv_tr(h)
                    if h + AHEAD < 16:
                        sc(h + AHEAD)
                        sm(h + AHEAD)
                    pv_mm(h)
                while pend_meta:
                    pend_meta.pop()()
                P.emit("DVE", lambda e, o=invs, i=rs16_t[bp]: e.reciprocal(o, i), reads=[rs16_t[bp]], writes=[invs], embed=True)
                for hb in range(2):
                    tt("DVE", otok[bp][:, hb * 512:(hb + 1) * 512].rearrange("p (h d) -> p h d", h=8),
                       sw_o[hb].rearrange("p (h d) -> p h d", h=8),
                       invs[:, hb * 8:(hb + 1) * 8].unsqueeze(2).to_broadcast([128, 8, 64]), ALU.mult)
                for jq in range(2):
                    for jj in range(4):
                        j = jq * 4 + jj
                        tr(oTps[jq][:, jj * 128:(jj + 1) * 128], otok[bp][:, j * 128:(j + 1) * 128], ident_b)
                    dst = b16v(o_B + jq * 4 * 1024, 4 * 512).rearrange("p (e t) -> p e t", e=4)[:, :, qc]
                    cp("DVE", dst, oTps[jq].rearrange("p (e t) -> p e t", e=4))
            for s in range(2):
                slab = ring_load(("swa", l, "o", s)).rearrange("p (i e c) -> p i e c", i=4, e=8)
                for ii in range(4):
                    i = s * 4 + ii
                    ps = acc[i % 2]
                    for jc in range(8):
                        mm(ps[:, :T], slab[:, ii, jc, :], kT[jc][:, :T], jc == 0, jc == 7)
                    resid_add(i, ps, T)

        def final(g):
            T = 512
            t0 = (g - 1) * 512
            norm(9, T, yT)
            if g + 1 < ngroups:
                load_input(g + 1, 512)
                debug_dump(g + 1, "in")
            o4 = ost.rearrange("p (c d) -> p c d", c=4)
            k = 0
            for c in range(4):
                for half in range(2):
                    ps = first4[k % 4]
                    for q in range(4):
                        dc = half * 4 + q
                        tr(ps[:, q * 128:(q + 1) * 128], yT[dc][:, c * 128:(c + 1) * 128], ident_f)
                    cp("ACT" if k % 2 else "DVE", o4[:, c, half * 512:(half + 1) * 512], ps[:, :512])
                    k += 1
            dst = out_d[t0:t0 + 512, :].rearrange("(c p) d -> p c d", p=128)

            def store():
                dma("SP", dst, o4, "ost", reads=[ost])
            return store

        try:
            for g in range(ngroups):
                T = 128 if g == 0 else 512
                if g == 1:
                    pass
                if g <= 1:
                    load_input(g, T)
                    debug_dump(g, "in")
                for l in range(min(2, nlayers)):
                    retention(l, g, T)
                    debug_dump(g, "mix%d" % l)
                    if l == 1 and g == 0 and ngroups > 1:
                        prefetch_x(1)
                    ffn(l, g, T)
                    debug_dump(g, "ffn%d" % l)
                if nlayers > 2:
                    kvproj(g, T)
                if g == 0:
                    continue
                for l in range(2, nlayers):
                    swa(l, g)
                    debug_dump(g, "mix%d" % l)
                    if l == 3 and g + 1 < ngroups:
                        prefetch_x(g + 1)
                    ffn(l, g, T)
                    debug_dump(g, "ffn%d" % l)
                if nlayers == 4:
                    PLe = "DVE" if (LAZY_CONV and g <= 1) else "POOL"
                    for k in range(4):
                        cp(PLe, KB[k][:, 16:144], KB[k][:, 528:656])
                    cp(PLe, Vt[1], Vt[5])
                st = final(g)
                if g + 1 < ngroups:
                    deferred.append(st)
                    defer_cnt[0] = 3
                else:
                    st()
            for f in deferred:
                f()
            deferred.clear()


        except _Stop:
            pass

        names = set(P.cnt.keys())
        sems = {}
        for nm in sorted(names):
            sems[nm] = es.enter_context(nc.semaphore("s_" + nm))
        block = es.enter_context(nc.Block())
        final_waits = [(nm, P.cnt[nm]) for nm in ("ost", "dbg") if nm in P.cnt]

        @block.sync
        def _(e):
            P.replay("SP", e, sems)
            for nm, v in final_waits:
                e.wait_ge(sems[nm], v)

        @block.tensor
        def _(e):
            P.replay("PE", e, sems)

        @block.scalar
        def _(e):
            P.replay("ACT", e, sems)

        @block.vector
        def _(e):
            P.replay("DVE", e, sems)

        @block.gpsimd
        def _(e):
            P.replay("POOL", e, sems)

    return nc, P


def make_in_maps(inputs):
    wsl = build_slabs(np.asarray(inputs["ret_w_in"], np.float32), np.asarray(inputs["ret_w_out"], np.float32),
                      np.asarray(inputs["kv_w"], np.float32), np.asarray(inputs["swa_w_q"], np.float32),
                      np.asarray(inputs["swa_w_o"], np.float32), np.asarray(inputs["ffn_w_in"], np.float32),
                      np.asarray(inputs["ffn_w_out"], np.float32))
    cst = build_consts(np.asarray(inputs["mix_norm"], np.float32), np.asarray(inputs["ffn_norm"], np.float32),
                       np.asarray(inputs["kv_norm"], np.float32), np.asarray(inputs["final_norm"], np.float32),
                       np.asarray(inputs["swa_sinks"], np.float32))
    x = np.asarray(inputs["x"], np.float32)
    meta = np.ascontiguousarray(np.asarray(inputs["meta_tokens"], np.float32))
    return [{"x": np.ascontiguousarray(x[b]), "meta": meta, "wsl": wsl, "cst": cst} for b in range(x.shape[0])]


def kernel(**inputs):
    in_maps = make_in_maps(inputs)
    nc, _ = build_program()
    res = run_bass_kernel_spmd(nc, in_maps, core_ids=list(range(8)))
    return np.stack([np.asarray(r["out"], np.float32) for r in res.results], axis=0)
```
